# Optimizing a Trainium2 kernel written in Bass

```python
import jax
import jax.numpy as jnp
from jax import lax
import numpy as np

D_MODEL = 2048
BATCH = 4
SEQ = 4096
DEPTH = 2

D_MIX = D_MODEL
N_MIXERS = 4
D_GROUP = D_MIX // N_MIXERS
N_IN_SLICES = 11
D_IN = N_IN_SLICES * D_GROUP
ATT_HEADS = 4
ATT_HEAD_DIM = D_GROUP // ATT_HEADS
MOBA_BLOCK = 256
MOBA_TOPK = 3
MOBA_Q_CHUNK = 64
MASK_VALUE = -1e30
RG_BLOCKS = 4
RG_BLOCK_DIM = D_GROUP // RG_BLOCKS
RG_CONV = 4
RG_C = 8.0
RG_A_MIN = 0.9
RG_A_MAX = 0.999
CV_WIDTH = 31
CV_GROUPS = 4
HG_HEADS = 4
HG_HEAD_DIM = D_GROUP // HG_HEADS
HG_CHUNK = 64
D_FF = 5632
LN_EPS = 1e-5
ALPHA = (2 * DEPTH) ** 0.25
BETA = (8 * DEPTH) ** -0.25

kernel_name = 'hybrid_moba_rglru_conformer_hgrn2_block'


def _standardize(x):
    x32 = x.astype(jnp.float32)
    mu = jnp.mean(x32, axis=-1, keepdims=True)
    var = jnp.mean(jnp.square(x32 - mu), axis=-1, keepdims=True)
    return (x32 - mu) * lax.rsqrt(var + LN_EPS)


def layer_norm(x, g, b):
    return (_standardize(x) * g + b).astype(x.dtype)


def swiglu(x, w_gate, w_up, w_down):
    return (jax.nn.silu(x @ w_gate) * (x @ w_up)) @ w_down


def causal_depthwise_conv(x, w, b):
    width, ch = w.shape
    y = lax.conv_general_dilated(
        x, w[:, None, :].astype(x.dtype), window_strides=(1,), padding=[(width - 1, 0)],
        dimension_numbers=('NWC', 'WIO', 'NWC'), feature_group_count=ch)
    return y + b


def split_heads(t, n_heads):
    bsz, seq, width = t.shape
    return t.reshape(bsz, seq, n_heads, width // n_heads)


def moba_attention(q, k, v):
    bsz, seq, nh, dh = q.shape
    nb = -(-seq // MOBA_BLOCK)
    s_pad = nb * MOBA_BLOCK
    pad = ((0, 0), (0, s_pad - seq), (0, 0), (0, 0))
    q, k, v = [jnp.pad(t, pad).transpose(0, 2, 1, 3) for t in (q, k, v)]
    kb = k.reshape(bsz, nh, nb, MOBA_BLOCK, dh)
    vb = v.reshape(bsz, nh, nb, MOBA_BLOCK, dh)
    k_mean = jnp.mean(kb, axis=3)
    topk = min(MOBA_TOPK, nb - 1)
    n_chunks = s_pad // MOBA_Q_CHUNK
    q_chunks = q.reshape(bsz, nh, n_chunks, MOBA_Q_CHUNK, dh).transpose(2, 0, 1, 3, 4)
    scale = dh ** -0.5
    b_idx = jnp.arange(bsz)[:, None, None, None]
    h_idx = jnp.arange(nh)[None, :, None, None]

    def one_chunk(args):
        c, q_c = args
        q_pos = c * MOBA_Q_CHUNK + jnp.arange(MOBA_Q_CHUNK)
        blk = (c * MOBA_Q_CHUNK) // MOBA_BLOCK
        k_own = lax.dynamic_index_in_dim(kb, blk, axis=2, keepdims=False)
        v_own = lax.dynamic_index_in_dim(vb, blk, axis=2, keepdims=False)
        k_pos = blk * MOBA_BLOCK + jnp.arange(MOBA_BLOCK)
        s_own = jnp.einsum('bhqd,bhkd->bhqk', q_c, k_own) * scale
        s_own = jnp.where(k_pos[None, :] <= q_pos[:, None], s_own, MASK_VALUE)
        if topk == 0:
            p_own = jax.nn.softmax(s_own.astype(jnp.float32), axis=-1)
            return jnp.einsum('bhqk,bhkd->bhqd', p_own, v_own)
        gate = jnp.einsum('bhqd,bhnd->bhqn', q_c, k_mean)
        gate = jnp.where(jnp.arange(nb) < blk, gate, MASK_VALUE)
        _, idx = lax.top_k(gate, topk)
        valid = idx < blk
        k_sel = kb[b_idx, h_idx, idx]
        v_sel = vb[b_idx, h_idx, idx]
        s_sel = jnp.einsum('bhqd,bhqnkd->bhqnk', q_c, k_sel) * scale
        s_sel = jnp.where(valid[..., None], s_sel, MASK_VALUE)
        s_all = jnp.concatenate(
            [s_own, s_sel.reshape(bsz, nh, MOBA_Q_CHUNK, topk * MOBA_BLOCK)], axis=-1)
        p = jax.nn.softmax(s_all.astype(jnp.float32), axis=-1)
        p_own = p[..., :MOBA_BLOCK]
        p_sel = p[..., MOBA_BLOCK:].reshape(bsz, nh, MOBA_Q_CHUNK, topk, MOBA_BLOCK)
        return (jnp.einsum('bhqk,bhkd->bhqd', p_own, v_own)
                + jnp.einsum('bhqnk,bhqnkd->bhqd', p_sel, v_sel))

    out = lax.map(one_chunk, (jnp.arange(n_chunks), q_chunks))
    out = out.transpose(1, 0, 3, 2, 4).reshape(bsz, s_pad, nh * dh)
    return out[:, :seq]


def _linear_recurrence_combine(left, right):
    a_l, b_l = left
    a_r, b_r = right
    return a_l * a_r, a_r * b_l + b_r


def rglru_mixer(gate_in, x_in, conv_w, conv_b, w_a, b_a, w_x, b_x, lam):
    bsz, seq, _ = x_in.shape
    xc = causal_depthwise_conv(x_in, conv_w, conv_b)
    xb = xc.reshape(bsz, seq, RG_BLOCKS, RG_BLOCK_DIM)
    r = jax.nn.sigmoid(jnp.einsum('bsgi,gio->bsgo', xb, w_a).reshape(bsz, seq, D_GROUP) + b_a)
    i = jax.nn.sigmoid(jnp.einsum('bsgi,gio->bsgo', xb, w_x).reshape(bsz, seq, D_GROUP) + b_x)
    log_a = -RG_C * r * jax.nn.softplus(-lam)
    a = jnp.exp(log_a)
    u = jnp.sqrt(jnp.maximum(-jnp.expm1(2.0 * log_a), 0.0)) * (i * xc)
    _, h = lax.associative_scan(_linear_recurrence_combine, (a, u), axis=1)
    return h * jax.nn.gelu(gate_in, approximate=True)


def conformer_conv_mixer(val, gate, conv_w, conv_b, norm_g, norm_b):
    bsz, seq, _ = val.shape
    u = causal_depthwise_conv(val * jax.nn.sigmoid(gate), conv_w, conv_b)
    u = _standardize(u.reshape(bsz, seq, CV_GROUPS, D_GROUP // CV_GROUPS)).reshape(bsz, seq, D_GROUP)
    return jax.nn.silu(u * norm_g + norm_b)


def hgrn2_mixer(q, f_logit, v, g, lb, norm_g):
    bsz, seq, _ = q.shape
    nc = seq // HG_CHUNK
    sig = jax.nn.sigmoid(f_logit)
    log_f = jnp.log(lb + (1.0 - lb) * sig)
    k = (1.0 - lb) * (1.0 - sig)

    def chunks(t):
        return t.reshape(bsz, nc, HG_CHUNK, HG_HEADS, HG_HEAD_DIM).transpose(1, 0, 3, 2, 4)

    causal = jnp.tril(jnp.ones((HG_CHUNK, HG_CHUNK), dtype=bool))[:, :, None]

    def step(state, inp):
        q_c, k_c, v_c, lf_c = inp
        b = jnp.cumsum(lf_c, axis=2)
        o_inter = jnp.einsum('bhtk,bhkv->bhtv', q_c * jnp.exp(b), state)
        diff = b[:, :, :, None, :] - b[:, :, None, :, :]
        decay = jnp.where(causal, jnp.exp(jnp.where(causal, diff, 0.0)), 0.0)
        att = jnp.einsum('bhtk,bhsk,bhtsk->bhts', q_c, k_c, decay)
        o = o_inter + jnp.einsum('bhts,bhsv->bhtv', att, v_c)
        b_last = b[:, :, -1:, :]
        state = (jnp.exp(b_last[:, :, 0, :, None]) * state
                 + jnp.einsum('bhsk,bhsv->bhkv', k_c * jnp.exp(b_last - b), v_c))
        return state, o

    s0 = jnp.zeros((bsz, HG_HEADS, HG_HEAD_DIM, HG_HEAD_DIM), jnp.float32)
    _, o = lax.scan(step, s0, (chunks(q), chunks(k), chunks(v), chunks(log_f)))
    o = o.transpose(1, 0, 3, 2, 4).reshape(bsz, seq, HG_HEADS, HG_HEAD_DIM)
    o = o * lax.rsqrt(jnp.mean(jnp.square(o), axis=-1, keepdims=True) + LN_EPS)
    o = o * norm_g.reshape(HG_HEADS, HG_HEAD_DIM)
    return o.reshape(bsz, seq, D_GROUP) * jax.nn.silu(g)


def setup_inputs(seed: int = 0) -> dict:
    key = jax.random.key(seed)
    ks = jax.random.split(key, 24)
    nrm = jax.random.normal
    f32 = jnp.float32
    x = nrm(ks[0], (BATCH, SEQ, D_MODEL), f32)
    ln_g = 1.0 + 0.02 * nrm(ks[1], (DEPTH, 3, D_MODEL), f32)
    ln_b = 0.02 * nrm(ks[2], (DEPTH, 3, D_MODEL), f32)
    ffn_w_gate = nrm(ks[3], (DEPTH, 2, D_MODEL, D_FF), f32) * D_MODEL ** -0.5
    ffn_w_up = nrm(ks[4], (DEPTH, 2, D_MODEL, D_FF), f32) * D_MODEL ** -0.5
    ffn_w_down = nrm(ks[5], (DEPTH, 2, D_FF, D_MODEL), f32) * (D_FF ** -0.5 * BETA)
    w_in = nrm(ks[6], (DEPTH, D_MODEL, D_IN), f32) * D_MODEL ** -0.5
    w_out = nrm(ks[7], (DEPTH, D_MIX, D_MODEL), f32) * (D_MIX ** -0.5 * BETA)
    rg_conv_w = nrm(ks[8], (DEPTH, RG_CONV, D_GROUP), f32) * RG_CONV ** -0.5
    rg_conv_b = 0.02 * nrm(ks[9], (DEPTH, D_GROUP), f32)
    rg_w_a = nrm(ks[10], (DEPTH, RG_BLOCKS, RG_BLOCK_DIM, RG_BLOCK_DIM), f32) * RG_BLOCK_DIM ** -0.5
    rg_b_a = 0.02 * nrm(ks[11], (DEPTH, D_GROUP), f32)
    rg_w_x = nrm(ks[12], (DEPTH, RG_BLOCKS, RG_BLOCK_DIM, RG_BLOCK_DIM), f32) * RG_BLOCK_DIM ** -0.5
    rg_b_x = 0.02 * nrm(ks[13], (DEPTH, D_GROUP), f32)
    a_c = jax.random.uniform(ks[14], (DEPTH, D_GROUP), f32, RG_A_MIN, RG_A_MAX)
    a0 = a_c ** (1.0 / RG_C)
    rg_lambda = jnp.log(a0) - jnp.log1p(-a0)
    cv_w = nrm(ks[15], (DEPTH, CV_WIDTH, D_GROUP), f32) * CV_WIDTH ** -0.5
    cv_b = 0.02 * nrm(ks[16], (DEPTH, D_GROUP), f32)
    cv_ln_g = 1.0 + 0.02 * nrm(ks[17], (DEPTH, D_GROUP), f32)
    cv_ln_b = 0.02 * nrm(ks[18], (DEPTH, D_GROUP), f32)
    hg_lower_bounds = 0.5 * nrm(ks[19], (DEPTH, D_GROUP), f32)
    hg_norm_g = 1.0 + 0.02 * nrm(ks[20], (DEPTH, D_GROUP), f32)
    return {'x': x, 'ln_g': ln_g, 'ln_b': ln_b, 'ffn_w_gate': ffn_w_gate, 'ffn_w_up': ffn_w_up,
            'ffn_w_down': ffn_w_down, 'w_in': w_in, 'w_out': w_out, 'rg_conv_w': rg_conv_w,
            'rg_conv_b': rg_conv_b, 'rg_w_a': rg_w_a, 'rg_b_a': rg_b_a, 'rg_w_x': rg_w_x,
            'rg_b_x': rg_b_x, 'rg_lambda': rg_lambda, 'cv_w': cv_w, 'cv_b': cv_b,
            'cv_ln_g': cv_ln_g, 'cv_ln_b': cv_ln_b, 'hg_lower_bounds': hg_lower_bounds,
            'hg_norm_g': hg_norm_g}


def reference(x, ln_g, ln_b, ffn_w_gate, ffn_w_up, ffn_w_down, w_in, w_out, rg_conv_w,
              rg_conv_b, rg_w_a, rg_b_a, rg_w_x, rg_b_x, rg_lambda, cv_w, cv_b, cv_ln_g,
              cv_ln_b, hg_lower_bounds, hg_norm_g):
    f32 = jnp.float32
    sm = jax.nn.softmax(hg_lower_bounds.astype(f32), axis=0)
    lower_bounds = jnp.cumsum(sm, axis=0) - sm[0:1]
    for l in range(DEPTH):
        x = layer_norm(ALPHA * x + 0.5 * swiglu(x, ffn_w_gate[l, 0], ffn_w_up[l, 0], ffn_w_down[l, 0]),
                       ln_g[l, 0], ln_b[l, 0])
        (a_q, a_k, a_v, b_gate, b_x, c_val, c_gate,
         d_q, d_f, d_i, d_g) = jnp.split((x @ w_in[l]).astype(f32), N_IN_SLICES, axis=-1)
        y_a = moba_attention(split_heads(a_q, ATT_HEADS), split_heads(a_k, ATT_HEADS),
                             split_heads(a_v, ATT_HEADS))
        y_b = rglru_mixer(b_gate, b_x, rg_conv_w[l], rg_conv_b[l], rg_w_a[l], rg_b_a[l],
                          rg_w_x[l], rg_b_x[l], rg_lambda[l])
        y_c = conformer_conv_mixer(c_val, c_gate, cv_w[l], cv_b[l], cv_ln_g[l], cv_ln_b[l])
        y_d = hgrn2_mixer(d_q, d_f, d_i, d_g, lower_bounds[l], hg_norm_g[l])
        y = jnp.concatenate([y_a, y_b, y_c, y_d], axis=-1).astype(x.dtype) @ w_out[l]
        x = layer_norm(ALPHA * x + y, ln_g[l, 1], ln_b[l, 1])
        x = layer_norm(ALPHA * x + 0.5 * swiglu(x, ffn_w_gate[l, 1], ffn_w_up[l, 1], ffn_w_down[l, 1]),
                       ln_g[l, 2], ln_b[l, 2])
    return x
```

```python
from contextlib import ExitStack

import os
import numpy as np
import concourse.bass as bass
import concourse.mybir as mybir
from concourse.bass_utils import run_bass_kernel_spmd

F32 = mybir.dt.float32
BF16 = mybir.dt.bfloat16
AF = mybir.ActivationFunctionType
ALU = mybir.AluOpType
AX = mybir.AxisListType

D_MODEL = 2048
BATCH = 4
SEQ = 4096
DEPTH = 2
D_GROUP = 512
D_FF = 5632
D_IN = 11 * D_GROUP
LN_EPS = 1e-5
ALPHA = (2 * DEPTH) ** 0.25
N_CORES = 8

DMA_SLOTS = 8


class _Op:
    __slots__ = ("eng", "fn", "reads", "writes", "is_dma", "deps", "signal", "semval",
                 "slot", "waits", "idx", "cc_inc", "slotq")

    def __init__(self, eng, fn, reads, writes, is_dma):
        self.eng = eng
        self.fn = fn
        self.reads = tuple(reads)
        self.writes = tuple(writes)
        self.is_dma = is_dma
        self.deps = []
        self.signal = False
        self.semval = None
        self.slot = None
        self.waits = []
        self.cc_inc = None
        self.slotq = None


class Sched:
    ENGS = ("pe", "act", "dve", "pool", "sp")

    def __init__(self, nc):
        self.nc = nc
        self.ops = []
        self.last_w = {}
        self.readers = {}

    def add(self, eng, fn, reads=(), writes=(), is_dma=False):
        op = _Op(eng, fn, reads, writes, is_dma)
        cur = getattr(self, "_cur", None)
        if cur is not None:
            self._streams[cur].append(op)
        else:
            self._commit(op)
        return op

    def _commit(self, op):
        op.idx = len(self.ops)
        deps = set()
        for k in op.reads:
            w = self.last_w.get(k)
            if w is not None:
                deps.add(w)
        for k in op.writes:
            w = self.last_w.get(k)
            if w is not None:
                deps.add(w)
            for r in self.readers.get(k, ()):
                deps.add(r)
        deps.discard(op.idx)
        op.deps = sorted(deps)
        for k in op.reads:
            self.readers.setdefault(k, []).append(op.idx)
        for k in op.writes:
            self.last_w[k] = op.idx
            self.readers[k] = []
        self.ops.append(op)

    def stream(self, name):
        if not hasattr(self, "_streams"):
            self._streams = {}
        self._streams.setdefault(name, [])
        self._cur = name

    def merge_streams(self):
        lists = [v for v in self._streams.values() if v]
        self._cur = None
        self._streams = {}
        pos = [0] * len(lists)
        while True:
            best, bf = None, None
            for i, l in enumerate(lists):
                if pos[i] < len(l):
                    fr = pos[i] / len(l)
                    if bf is None or fr < bf:
                        best, bf = i, fr
            if best is None:
                break
            self._commit(lists[best][pos[best]])
            pos[best] += 1

    def dma(self, queue, fn, reads=(), writes=(), slotq=None):
        op = self.add(queue, fn, reads, writes, is_dma=True)
        op.slotq = slotq
        return op

    def cc(self, queue, fn, reads=(), writes=(), inc=16):
        op = self.add(queue, fn, reads, writes, is_dma=True)
        op.cc_inc = inc
        return op

    def barrier(self):
        start = getattr(self, "bar_start", 0)
        last = {}
        dmas = []
        carry_ops = getattr(self, "_carry", [])
        for op in self.ops[start:]:
            if op.is_dma:
                if op.cc_inc is not None:
                    carry_ops.append(op)
                else:
                    dmas.append(op)
            elif op.fn is not None:
                last[op.eng] = op
        deps = list(last.values()) + dmas
        for e in self.ENGS:
            self.wait_all(e, deps)
        keep = {}
        live = []
        for op in carry_ops:
            ks = [k for k in op.writes if self.last_w.get(k) == op.idx]
            for k in ks:
                keep[k] = op.idx
            if ks:
                live.append(op)
        self._carry = live
        self.last_w.clear()
        self.readers.clear()
        self.last_w.update(keep)
        self.bar_start = len(self.ops)

    def wait_all(self, eng, ops):
        op = _Op(eng, None, (), (), False)
        op.idx = len(self.ops)
        op.deps = sorted(o.idx for o in ops)
        self.ops.append(op)
        return op

    def emit(self, stack):
        nc = self.nc
        ops = self.ops
        dma_count = {}
        slot_hist = {}
        slot_val = {}
        for op in ops:
            if op.is_dma:
                qn = (op.slotq or op.eng) if op.cc_inc is None else "cc"
                n = dma_count.get(qn, 0)
                dma_count[qn] = n + 1
                op.slot = (qn, n % (DMA_SLOTS if op.cc_inc is None else 2))
                slot_val[op.slot] = slot_val.get(op.slot, 0) + (16 if op.cc_inc is None else op.cc_inc)
                op.semval = slot_val[op.slot]
                prev = slot_hist.get(op.slot)
                if prev is not None and prev not in op.deps:
                    op.deps.append(prev)
                slot_hist[op.slot] = op.idx
        for op in ops:
            for d in op.deps:
                p = ops[d]
                if p.is_dma:
                    continue
                if p.eng == "pe" and op.eng == "pe" and not op.is_dma:
                    continue
                p.signal = True
        cnt = {e: 0 for e in self.ENGS}
        for op in ops:
            if not op.is_dma and op.signal:
                cnt[op.eng] += 1
                op.semval = cnt[op.eng]
        sems = {}
        for e in self.ENGS:
            sems[("c", e)] = stack.enter_context(nc.semaphore("sem_" + e))
        for q in dma_count:
            for s in range(DMA_SLOTS):
                sems[("d", q, s)] = stack.enter_context(nc.semaphore("dsem_%s_%d" % (q, s)))
        seen = {e: {} for e in self.ENGS}
        per_eng = {e: [] for e in self.ENGS}
        for op in ops:
            for d in op.deps:
                p = ops[d]
                if p.is_dma:
                    key = ("d",) + p.slot
                else:
                    if p.eng == "pe" and op.eng == "pe" and not op.is_dma:
                        continue
                    key = ("c", p.eng)
                val = p.semval
                if seen[op.eng].get(key, 0) >= val:
                    continue
                seen[op.eng][key] = val
                op.waits.append((key, val))
            per_eng[op.eng].append(op)
        self.n_instr = {e: len(per_eng[e]) for e in self.ENGS}

        def run(eng_obj, lst, ename):
            for op in lst:
                for key, val in op.waits:
                    eng_obj.wait_ge(sems[key], val)
                if op.fn is None:
                    continue
                inst = op.fn(eng_obj)
                if op.is_dma:
                    inst.then_inc(sems[("d",) + op.slot], 16 if op.cc_inc is None else op.cc_inc)
                elif op.signal:
                    inst.then_inc(sems[("c", ename)], 1)

        block = stack.enter_context(nc.Block())

        @block.tensor
        def _(e):
            run(e, per_eng["pe"], "pe")

        @block.scalar
        def _(e):
            run(e, per_eng["act"], "act")

        @block.vector
        def _(e):
            run(e, per_eng["dve"], "dve")

        @block.gpsimd
        def _(e):
            run(e, per_eng["pool"], "pool")

        @block.sync
        def _(e):
            run(e, per_eng["sp"], "sp")


class Ring:
    def __init__(self, name, tiles):
        self.name = name
        self.tiles = tiles
        self.i = 0

    def next(self):
        s = self.i % len(self.tiles)
        self.i += 1
        return self.tiles[s], (self.name, s)


class Arena:
    def __init__(self, t, ncols):
        self.t = t
        self.ncols = ncols
        self.off = 0

    def reset(self):
        self.off = 0

    def __call__(self, name, shape, dt=F32):
        n = 1
        for s in shape[1:]:
            n *= s
        w = n * (2 if dt == F32 else 1)
        w = (w + 15) // 16 * 16
        assert self.off + w <= self.ncols, ("SBUF arena overflow", name, self.off, w)
        v = self.t[:, self.off:self.off + (n * 2 if dt == F32 else n)]
        self.off += w
        if dt == F32:
            v = v.bitcast(F32)
        if len(shape) == 3:
            v = v.rearrange("p (a b) -> p a b", a=shape[1])
        elif len(shape) == 4:
            v = v.rearrange("p (a b c) -> p a b c", a=shape[1], b=shape[2])
        return v


TB = 512


def _ln_tail(S, z, ztag, st, lnw, out_ap, out_key, out_ops, tmpn, xt=None, oq="sp", g_eng="dve"):
    stats, mv, sd, nmr = st
    kst = ("lnstat", tmpn)
    for c in range(4):
        S.add("dve", lambda e, c=c: e.bn_stats(stats[:, c * 6:(c + 1) * 6], z[:, c * 512:(c + 1) * 512]),
              reads=[ztag], writes=[kst])
    S.add("dve", lambda e: e.bn_aggr(mv[:, :], stats[:, :]), reads=[kst], writes=[("lnmv", tmpn)])
    S.add("act", lambda e: e.activation(sd[:, 0:1], mv[:, 1:2], AF.Sqrt, bias=lnw["eps"][:, 0:1], scale=1.0),
          reads=[("lnmv", tmpn), "lnw"], writes=[("lnsd", tmpn)])
    S.add("dve", lambda e: e.reciprocal(sd[:, 1:2], sd[:, 0:1]), reads=[("lnsd", tmpn)], writes=[("lnrs", tmpn)])
    S.add("dve", lambda e: e.scalar_tensor_tensor(nmr[:, 0:1], mv[:, 0:1], -1.0, sd[:, 1:2], ALU.mult, ALU.mult),
          reads=[("lnmv", tmpn), ("lnrs", tmpn)], writes=[("lnnm", tmpn)])
    S.add("act", lambda e: e.activation(z[:, :], z[:, :], AF.Identity, bias=nmr[:, 0:1], scale=sd[:, 1:2]),
          reads=[ztag, ("lnrs", tmpn), ("lnnm", tmpn)], writes=[ztag])
    S.add(g_eng, lambda e: e.tensor_tensor(z[:, :], z[:, :], lnw["g"][:, :], ALU.mult),
          reads=[ztag, "lnw"], writes=[ztag])
    S.add("dve", lambda e: e.tensor_tensor(z[:, :], z[:, :], lnw["b"][:, :], ALU.add),
          reads=[ztag, "lnw"], writes=[ztag])
    o = S.dma(oq, lambda e: e.dma_start(out=out_ap, in_=z[:, :]), reads=[ztag], writes=[out_key])
    out_ops.append(o)
    if xt is not None:
        _ln_transposes(S, z, ztag, xt)


def _ln_transposes(S, z, ztag, xt):
    if True:
        ps, ident, stage, t = xt["ps"], xt["ident"], xt["stage"], xt["t"]
        banks = xt.get("banks", (0, 1, 2, 3))
        for q in range(4):
            bq = banks[q % len(banks)]
            for jj in range(4):
                kt = 4 * q + jj
                S.add("pe", lambda e, bq=bq, jj=jj, kt=kt: e.transpose(
                    ps[bq][:, jj * 128:(jj + 1) * 128], z[:, kt * 128:(kt + 1) * 128], ident),
                    reads=[ztag, "ident"], writes=[("ps", bq)])
            S.add("act", lambda e, q=q, bq=bq: e.copy(stage[:, 4 * q:4 * q + 4, t * 128:(t + 1) * 128],
                                                      ps[bq][:, :].rearrange("p (j c) -> p j c", j=4)),
                  reads=[("ps", bq)], writes=["xstage"])


def _alloc_ln(sb, nstat):
    lnw = {"g": sb("lng_s", [128, D_MODEL], F32), "b": sb("lnb_s", [128, D_MODEL], F32),
           "eps": sb("eps_s", [128, 1], F32)}
    stt = [(sb("stats%d" % i, [128, 24], F32), sb("mv%d" % i, [128, 2], F32),
            sb("sd%d" % i, [128, 2], F32), sb("nmr%d" % i, [128, 1], F32)) for i in range(nstat)]
    return lnw, stt


def _load_ln(S, lnw, lng, lnb):
    S.dma("sp", lambda e: e.dma_start(out=lnw["g"][:, :], in_=lng.partition_broadcast(128)), writes=["lnw"])
    S.dma("sp", lambda e: e.dma_start(out=lnw["b"][:, :], in_=lnb.partition_broadcast(128)), writes=["lnw"])
    S.add("dve", lambda e: e.memset(lnw["eps"][:, :], LN_EPS / (ALPHA * ALPHA)), writes=["lnw"])


def _down_phase(S, ps, wring, zt, xs, stt, lnw, lhs_fn, lhs_key, chunks, wd_v, coef, xtok, out, t0, cnt, out_ops,
                xo=None, wloader=None, defer=None):
    nk_last = chunks[-1][1] - 1
    for n in range(4):
        pbase = (n % 2) * 4
        for (k0, k1) in chunks:
            wt, wk = wring.next()
            nk = k1 - k0
            wv = wt[:, :nk * 512].rearrange("p (kt c) -> p kt c", kt=nk)
            if wloader is not None:
                wloader(wt, wk, n, k0, k1)
            else:
                S.dma("pool", lambda e, wv=wv, k0=k0, k1=k1, n=n: e.dma_start(
                    out=wv, in_=wd_v[:, k0:k1, n * 512:(n + 1) * 512]), writes=[wk])
            for t in range(4):
                for kt in range(k0, k1):
                    S.add("pe", lambda e, t=t, kt=kt, wv=wv, k0=k0, pbase=pbase: e.matmul(
                        ps[pbase + t][:, :], lhs_fn(kt, t), wv[:, kt - k0, :],
                        start=(kt == 0), stop=(kt == nk_last)),
                        reads=[wk, lhs_key(kt)], writes=[("ps", pbase + t)])
        for t in range(4):
            x_s, xk = xs[cnt["xsi"] % 2], ("xs", cnt["xsi"] % 2)
            cnt["xsi"] += 1
            S.dma("sp", lambda e, x_s=x_s, t=t, n=n: e.dma_start(
                out=x_s[:, :], in_=xtok[t0 + t * 128:t0 + (t + 1) * 128, n * 512:(n + 1) * 512]),
                reads=[("dram", "xres_in")], writes=[xk])
            S.add("dve", lambda e, x_s=x_s, t=t, n=n, pbase=pbase: e.scalar_tensor_tensor(
                zt[t][:, n * 512:(n + 1) * 512], ps[pbase + t][:, :], coef, x_s[:, :],
                ALU.mult, ALU.add),
                reads=[("ps", pbase + t), xk], writes=[("z", t)])
            if n == 3:
                tg = t0 + t * 128
                xt = None
                if xo is not None:
                    xt = {"ps": ps, "ident": xo["ident"], "stage": xo["stage"], "t": t}
                    if defer is not None:
                        xt["banks"] = (6, 7)
                oq = "pool" if defer is not None else "sp"
                li = cnt["lni"] % 2
                cnt["lni"] += 1

                def tail(t=t, tg=tg, xt=xt, oq=oq, li=li):
                    _ln_tail(S, zt[t], ("z", t), stt[li], lnw, out[tg:tg + 128, :],
                             ("dram_out", tg), out_ops, li, xt, oq)
                if defer is not None:
                    defer.append(tail)
                else:
                    tail()

    def fin():
        if xo is not None:
            oq = "pool" if defer is not None else "sp"
            o = S.dma(oq, lambda e: e.dma_start(out=xo["xTb"].rearrange("(kt p) t -> p kt t", p=128), in_=xo["stage"][:, :, :]),
                      reads=["xstage"], writes=[xo["key"]])
            out_ops.append(o)
            if xo.get("cc") is not None:
                csrc, cdst, ckey = xo["cc"]
                S.cc("pool", lambda e: e.collective_compute("AllGather", ALU.bypass, replica_groups=PAIRS,
                                                             ins=[csrc], outs=[cdst]),
                     reads=[xo["key"]], writes=[ckey], inc=1)
    if defer is not None:
        defer.append(fin)
    else:
        fin()


PAIRS = [[0, 1], [2, 3], [4, 5], [6, 7]]


def emit_ffn(S, sb, ps, c):
    nb = c["ntok"] // TB
    wg_v = c["wg"].rearrange("(kt p) c -> p kt c", p=128)
    wu_v = c["wu"].rearrange("(kt p) c -> p kt c", p=128)
    wd_v = c["wd"].rearrange("(kt p) c -> p kt c", p=128)
    xTs = sb("xTs", [128, 16, TB], BF16)
    hT = sb("hT", [128, 44, TB], BF16)
    wring = Ring("w", [sb("wr%d" % i, [128, 8192], BF16) for i in range(4)])
    sig = [sb("sig%d" % i, [128, TB], F32) for i in range(2)]
    xs = [sb("xs%d" % i, [128, 512], F32) for i in range(2)]
    zt = [sb("z%d" % i, [128, D_MODEL], F32) for i in range(4)]
    lnw, stt = _alloc_ln(sb, 2)
    stage = sb("xstage", [128, 16, TB], BF16)
    ident = sb("identf", [128, 128], F32)
    S.dma("sp", lambda e: e.dma_start(out=ident[:, :], in_=c["consts"][:, C_ID:C_ID + 128]), writes=["ident"])
    _load_ln(S, lnw, c["lng"], c["lnb"])
    out_ops = c["out_ops"]
    psi = [0]
    cnt = {"xsi": 0, "lni": 0}
    use_defer = c.get("ws") is not None
    pending = []

    def load_x(bt):
        q, xap, xkey = c["xT_src"](bt)
        if q == "sp" and bt > 0:
            q = "act"
        S.dma(q, lambda e, xap=xap: e.dma_start(out=xTs[:, :, :], in_=xap), reads=[xkey], writes=["xTs"])

    bg = c.get("bg") or []

    def gate_up(bt):
        for cp in range(22):
            if bg and cp % 3 == 0:
                bg.pop(0)()
            wt, wk = wring.next()
            wv = wt[:, :].rearrange("p (m kt c) -> p m kt c", m=2, kt=16)
            ws = c.get("ws")
            if ws is None or (bt == 0 and not c.get("ws_ready")):
                S.dma("pool", lambda e, wv=wv, cp=cp: e.dma_start(out=wv[:, 0], in_=wg_v[:, :, cp * 256:(cp + 1) * 256]),
                      writes=[wk])
                S.dma("pool", lambda e, wv=wv, cp=cp: e.dma_start(out=wv[:, 1], in_=wu_v[:, :, cp * 256:(cp + 1) * 256]),
                      writes=[wk])
                if ws is not None:
                    S.dma("sp", lambda e, wt=wt, cp=cp: e.dma_start(out=ws[cp], in_=wt[:, :]), reads=[wk],
                          writes=[("dram", "WS", cp)])
            else:
                S.dma("sp", lambda e, wt=wt, cp=cp: e.dma_start(out=wt[:, :], in_=ws[cp]), reads=[("dram", "WS", cp)],
                      writes=[wk])
            for ci in range(2):
                cc_ = cp * 2 + ci
                b0, b1 = psi[0] % 6, (psi[0] + 1) % 6
                pg, pu = ps[b0], ps[b1]
                kg, ku = ("ps", b0), ("ps", b1)
                psi[0] += 2
                for m, (pp_, kk) in enumerate(((pg, kg), (pu, ku))):
                    for kt in range(16):
                        S.add("pe", lambda e, pp_=pp_, wv=wv, m=m, kt=kt, ci=ci: e.matmul(
                            pp_[:, :], wv[:, m, kt, ci * 128:(ci + 1) * 128], xTs[:, kt, :],
                            start=(kt == 0), stop=(kt == 15)),
                            reads=[wk, "xTs"], writes=[kk])
                sg = sig[cc_ % 2]
                S.add("act", lambda e, sg=sg, pg=pg: e.activation(sg[:, :], pg[:, :], AF.Silu),
                      reads=[kg], writes=[("sig", cc_ % 2)])
                S.add("dve", lambda e, sg=sg, pu=pu, cc_=cc_: e.tensor_tensor(hT[:, cc_, :], sg[:, :], pu[:, :], ALU.mult),
                      reads=[("sig", cc_ % 2), ku], writes=[("hT", cc_)])

    load_x(0)
    for bt in range(nb):
        t0 = bt * TB
        if pending:
            S.stream("T")
            for fn_ in pending:
                fn_()
            pending = []
            S.stream("G")
            gate_up(bt)
            S.merge_streams()
        else:
            gate_up(bt)
        if bt + 1 < nb:
            load_x(bt + 1)
        xo = c["xo"](bt)
        if xo is not None:
            xo = dict(xo, ident=ident[:, :], stage=stage)
        wloader = None
        if c.get("ws") is not None:
            def wloader(wt, wk, n, k0, k1, bt=bt):
                ws = c["ws"]
                si = 22 + n * 3 + k0 // 16
                nk = k1 - k0
                if bt == 0 and not c.get("ws_ready"):
                    wv = wt[:, :nk * 512].rearrange("p (kt c) -> p kt c", kt=nk)
                    S.dma("pool", lambda e: e.dma_start(out=wv, in_=wd_v[:, k0:k1, n * 512:(n + 1) * 512]), writes=[wk])
                    S.dma("sp", lambda e: e.dma_start(out=ws[si][:, :nk * 512], in_=wt[:, :nk * 512]), reads=[wk],
                          writes=[("dram", "WS", si)])
                else:
                    S.dma("sp", lambda e: e.dma_start(out=wt[:, :nk * 512], in_=ws[si][:, :nk * 512]),
                          reads=[("dram", "WS", si)], writes=[wk])
        defer = pending if use_defer else None
        _down_phase(S, ps, wring, zt, xs, stt, lnw, lambda kt, t: hT[:, kt, t * 128:(t + 1) * 128],
                    lambda kt: ("hT", kt), ((0, 16), (16, 32), (32, 44)), wd_v, 0.5 / ALPHA, c["xres_in"], c["out_res"],
                    t0, cnt, out_ops, xo, wloader, defer)
    for fn_ in pending:
        fn_()
    while bg:
        bg.pop(0)()


def emit_wout(S, sb, ps, c):
    nb = c["ntok"] // TB
    wo_v = c["wo"].rearrange("(kt p) c -> p kt c", p=128)
    yTs = [sb("yTs%d" % i, [128, 16, TB], BF16) for i in range(2)]
    yalt = sb("yalt", [128, 16, TB], BF16)
    wres = [sb("wr%d" % i, [128, 16, 512], BF16) for i in range(4)]
    xs = [sb("xs%d" % i, [128, 512], F32) for i in range(4)]
    zt = [sb("z%d" % i, [128, D_MODEL], F32) for i in range(4)]
    lnw, stt = _alloc_ln(sb, 2)
    stage = sb("xstage", [128, 16, TB], BF16)
    ident = sb("identf", [128, 128], F32)
    flg = sb("flags", [128, 2], F32)
    S.dma("sp", lambda e: e.dma_start(out=ident[:, :], in_=c["consts"][:, C_ID:C_ID + 128]), writes=["ident"])
    S.dma("sp", lambda e: e.dma_start(out=flg[:, :], in_=c["flags"]), writes=["flags"])
    _load_ln(S, lnw, c["lng"], c["lnb"])
    for n in range(4):
        if c.get("wso") is not None:
            S.dma("sp", lambda e, n=n: e.dma_start(out=wres[n][:, :, :],
                                                   in_=c["wso"][n].rearrange("p (kt c) -> p kt c", kt=16)),
                  writes=[("wres", n)])
        else:
            S.dma("pool", lambda e, n=n: e.dma_start(out=wres[n][:, :, :], in_=wo_v[:, :, n * 512:(n + 1) * 512]),
                  writes=[("wres", n)])
    out_ops = c["out_ops"]
    bg = c.get("bg") or []
    coef = 1.0 / ALPHA
    pend = []
    pcount = 0
    xsi = 0
    lni = 0

    def load_y(bt):
        yb = yTs[bt % 2]
        y0, y0k = c["yg"](bt)
        y1, y1k = c["yg"](4 + bt)
        S.dma("sp", lambda e, yb=yb, y0=y0: e.dma_start(out=yb[:, :, :], in_=y0.rearrange("(kt p) t -> p kt t", p=128)),
              reads=[y0k], writes=[("yTs", bt % 2)])
        S.dma("sp", lambda e, y1=y1: e.dma_start(out=yalt[:, :, :], in_=y1.rearrange("(kt p) t -> p kt t", p=128)),
              reads=[y1k], writes=["yalt"])
        S.add("dve", lambda e: e.tensor_scalar(yalt[:, :, :], yalt[:, :, :], flg[:, 1:2], None, ALU.mult),
              reads=["yalt", "flags"], writes=["yalt"])
        S.add("dve", lambda e, yb=yb: e.scalar_tensor_tensor(yb[:, :, :], yb[:, :, :], flg[:, 0:1], yalt[:, :, :],
                                                              ALU.mult, ALU.add),
              reads=["yalt", "flags", ("yTs", bt % 2)], writes=[("yTs", bt % 2)])

    load_y(0)
    for bt in range(nb):
        t0 = bt * TB
        yb = yTs[bt % 2]
        if bt + 1 < nb:
            load_y(bt + 1)
        for _ in range((len(bg) + (nb - bt) - 1) // (nb - bt) if bg else 0):
            bg.pop(0)()
        xo = c["xo"](bt)
        for t in range(4):
            z = zt[t]
            for n in range(4):
                bk = pcount % 6
                pcount += 1
                for kt in range(16):
                    S.add("pe", lambda e, bk=bk, kt=kt, t=t, n=n, yb=yb: e.matmul(
                        ps[bk][:, :], yb[:, kt, t * 128:(t + 1) * 128], wres[n][:, kt, :],
                        start=(kt == 0), stop=(kt == 15)),
                        reads=[("wres", n), ("yTs", bt % 2)], writes=[("ps", bk)])
                x_s, xk = xs[xsi % 4], ("xs", xsi % 4)
                xsi += 1
                S.dma("sp", lambda e, x_s=x_s, t=t, n=n, t0=t0: e.dma_start(
                    out=x_s[:, :], in_=c["xres_in"][t0 + t * 128:t0 + (t + 1) * 128, n * 512:(n + 1) * 512]),
                    reads=[("dram", "xres_in")], writes=[xk])
                S.add("act", lambda e, z=z, n=n, bk=bk: e.mul(z[:, n * 512:(n + 1) * 512], ps[bk][:, :], coef),
                      reads=[("ps", bk)], writes=[("z", t)])
                S.add("pool", lambda e, x_s=x_s, z=z, n=n: e.tensor_tensor(
                    z[:, n * 512:(n + 1) * 512], z[:, n * 512:(n + 1) * 512], x_s[:, :], ALU.add),
                    reads=[("z", t), xk], writes=[("z", t)])
            tg = t0 + t * 128
            xt = None
            if xo is not None:
                xt = {"ps": ps, "ident": ident[:, :], "stage": stage, "t": t, "banks": (6, 7)}
            li = lni % 2
            lni += 1

            _ln_tail(S, z, ("z", t), stt[li], lnw, c["out_res"][tg:tg + 128, :], ("dram_out", tg), out_ops, li, None,
                     "pool", "pool")

            def tail(t=t, tg=tg, xt=xt, li=li, z=z, last=(t == 3), xo=xo):
                if xt is not None:
                    _ln_transposes(S, z, ("z", t), xt)
                if last and xo is not None:
                    o = S.dma("pool", lambda e: e.dma_start(out=xo["xTb"].rearrange("(kt p) t -> p kt t", p=128),
                                                          in_=stage[:, :, :]), reads=["xstage"], writes=[xo["key"]])
                    out_ops.append(o)
            pend.append(tail)
            if len(pend) > 2:
                pend.pop(0)()
    while pend:
        pend.pop(0)()
    while bg:
        bg.pop(0)()


GT = 512
HC = 32
MASKV = -1e30
PP_RG = 0
PP_CV = 16
PP_CVW = 22
PP_HG = 84
NPP = 90
C_ID, C_TRI, C_BD, C_CM, C_RM = 0, 128, 256, 384, 896
NCONST = 900


def make_consts():
    c = np.zeros((128, NCONST), np.float32)
    p = np.arange(128)[:, None]
    f = np.arange(128)[None, :]
    c[:, C_ID:C_ID + 128] = (p == f)
    c[:, C_TRI:C_TRI + 128] = (p <= f)
    c[:, C_BD:C_BD + 128] = (p <= f) & ((p // HC) == (f // HC))
    t = np.arange(512)
    c[:, C_CM:C_CM + 512] = (t % HC != 0)[None, :]
    for k in range(4):
        c[:, C_RM + k] = ((np.arange(128) // HC) == k)
    return c


def emit_mixer(S, sb, ps, cfg, parts="ABCD"):
    seq = cfg["seq"]
    ng = seq // GT
    win_v = cfg["win"].rearrange("(kt p) c -> p kt c", p=128)
    pp_d, cst_d, rgw_d, hgn_d = cfg["pp"], cfg["consts"], cfg["rgw"], cfg["hgn"]
    scale = 128 ** -0.5
    out_ops = cfg["out_ops"]
    if True:
        if True:
            pass
        pp = sb("pp", [128, NPP])
        cst = sb("cst", [128, NCONST])
        S.dma("sp", lambda e: e.dma_start(out=pp[:, :], in_=pp_d), writes=["pp"])
        S.dma("sp", lambda e: e.dma_start(out=cst[:, :], in_=cst_d), writes=["cst"])
        identb = sb("identb", [128, 128], BF16)
        trib = sb("trib", [128, 128], BF16)
        bdm = sb("bdm", [128, 128])
        ones_f = sb("ones_f", [128, 128])
        S.add("dve", lambda e: e.tensor_copy(identb[:, :], cst[:, C_ID:C_ID + 128]), reads=["cst"], writes=["identb"])
        S.add("dve", lambda e: e.tensor_copy(trib[:, :], cst[:, C_TRI:C_TRI + 128]), reads=["cst"], writes=["trib"])
        S.add("dve", lambda e: e.tensor_copy(bdm[:, :], cst[:, C_BD:C_BD + 128]), reads=["cst"], writes=["bdm"])
        S.add("dve", lambda e: e.memset(ones_f[:, :], 1.0 / 128.0), writes=["ones_f"])
        cmask = cst[:, C_CM:C_CM + 512]
        rgw = sb("rgw", [128, 4, 128], BF16)
        S.dma("pool", lambda e: e.dma_start(out=rgw[:, :, :], in_=rgw_d.rearrange("p (m o) -> p m o", m=4)),
              writes=["rgw"])
        hgn = sb("hgn", [128, 256])
        S.dma("sp", lambda e: e.dma_start(out=hgn[:, :], in_=hgn_d.partition_broadcast(128)), writes=["hgn"])
        der = sb("der", [128, 16])
        tmpd = sb("tmpd", [128, 8])
        S.add("dve", lambda e: e.memset(der[:, 10:11], LN_EPS), writes=["der_c"])
        S.add("dve", lambda e: e.memset(der[:, 11:12], 1.0), writes=["der_c"])
        for j in range(2):
            lam = pp[:, PP_RG + 8 * j + 7:PP_RG + 8 * j + 8]
            S.add("act", lambda e, j=j, lam=lam: e.activation(tmpd[:, j:j + 1], lam, AF.Exp, scale=-1.0),
                  reads=["pp"], writes=[("tmpd", j)])
            S.add("act", lambda e, j=j: e.activation(tmpd[:, 2 + j:3 + j], tmpd[:, j:j + 1], AF.Ln, bias=der[:, 11:12], scale=1.0),
                  reads=[("tmpd", j), "der_c"], writes=[("tmpd2", j)])
            S.add("dve", lambda e, j=j: e.tensor_scalar(der[:, j:j + 1], tmpd[:, 2 + j:3 + j], -8.0, None, ALU.mult),
                  reads=[("tmpd2", j)], writes=["der"])
            S.add("dve", lambda e, j=j: e.tensor_scalar(der[:, 2 + j:3 + j], tmpd[:, 2 + j:3 + j], -16.0, None, ALU.mult),
                  reads=[("tmpd2", j)], writes=["der"])
            l0 = pp[:, PP_HG + 3 * j:PP_HG + 3 * j + 1]
            l1 = pp[:, PP_HG + 3 * j + 1:PP_HG + 3 * j + 2]
            fl = pp[:, PP_HG + 3 * j + 2:PP_HG + 3 * j + 3]
            S.add("dve", lambda e, j=j, l0=l0, l1=l1: e.tensor_tensor(tmpd[:, 4 + j:5 + j], l1, l0, ALU.subtract),
                  reads=["pp"], writes=[("tmpd3", j)])
            S.add("act", lambda e, j=j: e.activation(tmpd[:, 6 + j:7 + j], tmpd[:, 4 + j:5 + j], AF.Sigmoid),
                  reads=[("tmpd3", j)], writes=[("tmpd4", j)])
            S.add("dve", lambda e, j=j, fl=fl: e.tensor_tensor(der[:, 4 + j:5 + j], tmpd[:, 6 + j:7 + j], fl, ALU.mult),
                  reads=[("tmpd4", j), "pp"], writes=["der"])
            S.add("dve", lambda e, j=j: e.tensor_scalar(der[:, 6 + j:7 + j], der[:, 4 + j:5 + j], -1.0, 1.0, ALU.mult, ALU.add),
                  reads=["der"], writes=["der"])
            S.add("dve", lambda e, j=j: e.tensor_scalar(der[:, 8 + j:9 + j], der[:, 4 + j:5 + j], 1.0, -1.0, ALU.mult, ALU.add),
                  reads=["der"], writes=["der"])
        cvd = sb("cvd", [128, 2, 31, 128], BF16)
        for j in range(2):
            for k in range(31):
                S.add("dve", lambda e, j=j, k=k: e.tensor_scalar(
                    cvd[:, j, k, :], cst[:, C_ID:C_ID + 128], pp[:, PP_CVW + 31 * j + k:PP_CVW + 31 * j + k + 1], None, ALU.mult),
                    reads=["cst", "pp"], writes=["cvd"])
        xTs2 = [sb("xTs%d" % i_, [128, 16, GT], BF16) for i_ in range(2)]
        wrings = {"X": Ring("wX", [sb("wrX%d" % i, [128, 16, 256], BF16) for i in range(3)]),
                  "Y": Ring("wY", [sb("wrY%d" % i, [128, 16, 256], BF16) for i in range(3)])}
        cur = {"s": "X"}
        kT = sb("kT", [128, 2, seq], BF16)
        vaug = sb("vaug", [128, 2, seq // 128, 130], BF16)
        kmT = sb("kmT", [128, 2, 16], BF16)
        qT = sb("qT", [128, 2, GT], BF16)
        S.add("dve", lambda e: e.memset(vaug[:, :, :, 128:130], 1.0), writes=["vaug_ones"])
        kmf = sb("kmf", [128, 2, 16])
        S.add("dve", lambda e: e.memset(kmf[:, :, :], 0.0), writes=[("kmf", 0), ("kmf", 1)])
        pring = Ring("pT", [sb("pT%d" % i, [128, GT], BF16) for i in range(4)])
        acc = sb("acc", [128, 4, 130])
        gsb = sb("gsb", [128, 16])
        top8 = sb("top8", [128, 8])
        sel = sb("sel", [128, 4, 16])
        ksum = sb("ksum", [128, 2])
        rcp = sb("rcp", [128, 4])
        ystgs = {s_: Ring("ystg" + s_, [sb("ystg%s%d" % (s_, i), [128, 512]) for i in range(1)]) for s_ in "XY"}
        frings = {"X": Ring("fX", [sb("fX%d" % i, [128, GT]) for i in range(8)]),
                  "Y": Ring("fY", [sb("fY%d" % i, [128, GT]) for i in range(7)])}
        brings = {"X": Ring("bX", [sb("bbX%d" % i, [128, GT], BF16) for i in range(2)]),
                  "Y": Ring("bY", [sb("bbY%d" % i, [128, GT], BF16) for i in range(3)])}
        rgx = sb("rgx", [128, 2, 3 + GT])
        rgh = sb("rgh", [128, 2, 8])
        S.add("dve", lambda e: e.memset(rgx[:, :, 0:3], 0.0), writes=[("rgx", 0), ("rgx", 1)])
        S.add("dve", lambda e: e.memset(rgh[:, :, 0:1], 0.0), writes=[("rgh", 0), ("rgh", 1)])
        cvx = sb("cvx", [128, 2, 30 + GT], BF16)
        S.add("dve", lambda e: e.memset(cvx[:, :, 0:30], 0.0), writes=[("cvx", 0), ("cvx", 1)])
        S32 = sb("S32", [128, 2, 128])
        Sb = sb("Sb", [128, 2, 128], BF16)
        S.add("dve", lambda e: e.memset(S32[:, :, :], 0.0), writes=[("S32", 0), ("S32", 1)])
        S.add("dve", lambda e: e.memset(Sb[:, :, :], 0.0), writes=[("Sb", 0), ("Sb", 1)])
        qdz = sb("qdz", [128, 4, 128], BF16)
        S.add("dve", lambda e: e.memset(qdz[:, :, :], 0.0), writes=["qdz"])
        khz = sb("khz", [128, 4, 128], BF16)
        vtok = sb("vtok", [128, 4, 128], BF16)
        gtok = sb("gtok", [128, 4, 128])
        attm = sb("attm", [128, 128], BF16)
        hsm = sb("hsm", [128, 8])
        prings = {("pj", "X"): Ring("ps", [(ps[0], 0)]), ("pm", "X"): Ring("ps", [(ps[1], 1), (ps[2], 2)]),
                  ("pa", "X"): Ring("ps", [(ps[3], 3), (ps[4], 4)]),
                  ("pj", "Y"): Ring("ps", [(ps[5], 5)]), ("pm", "Y"): Ring("ps", [(ps[6], 6)]),
                  ("pa", "Y"): Ring("ps", [(ps[7], 7)])}
        pj, pm, pa = "pj", "pm", "pa"

        def psn(ring):
            (t, i), _ = prings[(ring, cur["s"])].next()
            return t, ("ps", i)

        wstates = {"X": {}, "Y": {}}

        def wtile(ct):
            cp = ct // 2
            wstate = wstates[cur["s"]]
            if cp not in wstate:
                wt, wk = wrings[cur["s"]].next()
                for k_ in [k_ for k_, v_ in wstate.items() if v_[1] == wk]:
                    del wstate[k_]
                S.dma("pool", lambda e, wt=wt, cp=cp: e.dma_start(out=wt[:, :, :], in_=win_v[:, :, cp * 256:(cp + 1) * 256]),
                      writes=[wk])
                wstate[cp] = (wt, wk)
            wt, wk = wstate[cp]
            return wt[:, :, (ct % 2) * 128:(ct % 2 + 1) * 128], wk

        def proj_fm(ct):
            w, wk = wtile(ct)
            p, pk = psn(pj)
            xTs, xkey_ = gstate["xTs"]
            for kt in range(16):
                S.add("pe", lambda e, p=p, w=w, kt=kt, xTs=xTs: e.matmul(p[:, :], w[:, kt, :], xTs[:, kt, :],
                                                                          start=(kt == 0), stop=(kt == 15)),
                      reads=[wk, xkey_], writes=[pk])
            return p, pk

        def proj_tm(ct):
            w, wk = wtile(ct)
            p, pk = psn(pj)
            xTs, xkey_ = gstate["xTs"]
            for tt in range(4):
                for kt in range(16):
                    S.add("pe", lambda e, p=p, w=w, kt=kt, tt=tt, xTs=xTs: e.matmul(
                        p[:, tt * 128:(tt + 1) * 128], xTs[:, kt, tt * 128:(tt + 1) * 128], w[:, kt, :],
                        start=(kt == 0), stop=(kt == 15)),
                        reads=[wk, xkey_], writes=[pk])
            return p, pk

        def fnext():
            return frings[cur["s"]].next()

        class _RingSel:
            def __init__(self, d):
                self.d = d

            def next(self):
                return self.d[cur["s"]].next()

        bring = _RingSel(brings)
        ystg = _RingSel(ystgs)
        ybst = _RingSel({s_: Ring("ybst" + s_, [sb("ybst%s%d" % (s_, i), [128, GT], BF16) for i in range(2)]) for s_ in "XY"})
        gstate = {}

        def emit_yfm(yb_, ybk_, ci):
            ysrc, ysk_d = gstate["ysrc"]
            o = S.dma("sp", lambda e, yb_=yb_, ysrc=ysrc, ci=ci: e.dma_start(out=ysrc[ci * 128:(ci + 1) * 128, :], in_=yb_[:, :]),
                      reads=[ybk_], writes=[ysk_d])
            out_ops.append(o)
            gstate["n"] += 1

        def emit_ytok(ys, ysk, ci):
            pt_, ptk_ = psn(pm)
            for i in range(4):
                S.add("pe", lambda e, pt_=pt_, ys=ys, i=i: e.transpose(pt_[:, i * 128:(i + 1) * 128], ys[:, i * 128:(i + 1) * 128],
                                                                        cst[:, C_ID:C_ID + 128]),
                      reads=[ysk, "cst"], writes=[ptk_])
            yb_, ybk_ = ybst.next()
            S.add("act", lambda e, yb_=yb_, pt_=pt_: e.copy(yb_[:, :], pt_[:, :]), reads=[ptk_], writes=[ybk_])
            emit_yfm(yb_, ybk_, ci)

        for gi in range(ng):
            t0 = gi * GT
            def load_x(g_):
                xap, xkey = cfg["xg"](g_)
                xb_ = xTs2[g_ % 2]
                S.dma("sp", lambda e, xap=xap, xb_=xb_: e.dma_start(out=xb_[:, :, :], in_=xap), reads=[xkey],
                      writes=[("xTs", g_ % 2)])
            if gi == 0:
                load_x(0)
            if gi + 1 < ng:
                load_x(gi + 1)
            gstate["xTs"] = (xTs2[gi % 2], ("xTs", gi % 2))
            gstate["ysrc"] = cfg["ysrc"](gi)
            gstate["n"] = 0
            def unit_A(hh, gi=gi, t0=t0):
                p, pk = proj_fm(0 + hh)
                S.add("act", lambda e, p=p, hh=hh: e.copy(qT[:, hh, :], p[:, :]), reads=[pk], writes=[("qT", hh)])
                p, pk = proj_fm(2 + hh)
                S.add("act", lambda e, p=p, hh=hh, t0=t0: e.copy(kT[:, hh, t0:t0 + GT], p[:, :]), reads=[pk],
                      writes=[("kT", hh, gi)])
                for b_ in range(2):
                    jk, jkk = fnext()
                    S.add("act", lambda e, p=p, jk=jk, b_=b_, hh=hh, gi=gi: e.activation(
                        jk[:, 0:256], p[:, b_ * 256:(b_ + 1) * 256], AF.Identity, scale=1.0 / 256.0,
                        accum_out=kmf[:, hh, 2 * gi + b_:2 * gi + b_ + 1]),
                        reads=[pk], writes=[jkk, ("kmf", hh)])
                S.add("dve", lambda e, hh=hh: e.tensor_copy(kmT[:, hh, :], kmf[:, hh, :]), reads=[("kmf", hh)],
                      writes=[("kmT", hh)])
                p, pk = proj_tm(4 + hh)
                if not os.environ.get("SKIP_V"):
                    S.add("act", lambda e, p=p, hh=hh, gi=gi: e.copy(
                        vaug[:, hh, 4 * gi:4 * gi + 4, 0:128], p[:, :].rearrange("p (t c) -> p t c", t=4)),
                        reads=[pk], writes=[("vaug", hh, gi)])
                MST = int(os.environ.get('MOBA_STAGE', '9'))
                if MST < 2:
                    return
                for i in range(4):
                    blk = 2 * gi + i // 2
                    S.add("dve", lambda e: e.memset(gsb[:, :], MASKV), writes=["gsb"])
                    if blk > 0:
                        pg, pgk = psn(pm)
                        S.add("pe", lambda e, pg=pg, hh=hh, i=i: e.matmul(pg[:, 0:16], qT[:, hh, i * 128:(i + 1) * 128],
                                                                        kmT[:, hh, :], start=True, stop=True),
                              reads=[("qT", hh), ("kmT", hh)], writes=[pgk])
                        S.add("dve", lambda e, pg=pg, blk=blk: e.tensor_copy(gsb[:, 0:blk], pg[:, 0:blk]),
                              reads=[pgk], writes=["gsb"])
                    S.add("dve", lambda e: e.max(top8[:, :], gsb[:, :]), reads=["gsb"], writes=["top8"])
                    S.add("dve", lambda e, i=i: e.tensor_scalar(sel[:, i, :], gsb[:, :], top8[:, 2:3], None, ALU.is_ge),
                          reads=["gsb", "top8"], writes=[("sel", i)])
                    S.add("dve", lambda e, i=i, blk=blk: e.memset(sel[:, i, blk:blk + 1], 1.0), writes=[("sel", i)])
                if MST < 3:
                    return
                S.add("dve", lambda e: e.memset(acc[:, :, :], 0.0), writes=["acc"])
                nblk = 2 * gi + 2
                for n in range(nblk):
                    pts = []
                    for jj in range(2):
                        kt = 2 * n + jj
                        psc, psk = psn(pm)
                        S.add("pe", lambda e, psc=psc, hh=hh, kt=kt: e.matmul(
                            psc[:, :], kT[:, hh, kt * 128:(kt + 1) * 128], qT[:, hh, :], start=True, stop=True),
                            reads=[("kT", hh, kt // 4), ("qT", hh)], writes=[psk])
                        pt, ptk = pring.next()
                        S.add("act", lambda e, pt=pt, psc=psc: e.activation(pt[:, :], psc[:, :], AF.Exp, scale=scale),
                              reads=[psk], writes=[ptk])
                        j = kt - 4 * gi
                        if j >= 0:
                            S.add("dve", lambda e, pt=pt, j=j: e.tensor_tensor(
                                pt[:, j * 128:(j + 1) * 128], pt[:, j * 128:(j + 1) * 128], trib[:, :], ALU.mult),
                                reads=[ptk, "trib"], writes=[ptk])
                        pts.append((pt, ptk, kt))
                    for pair in (range(2) if MST >= 4 else ()):
                        po, pok = psn(pa)
                        used = []
                        for ii in range(2):
                            i = pair * 2 + ii
                            kts = [(pt, ptk, kt) for (pt, ptk, kt) in pts if kt <= 4 * gi + i]
                            for idx, (pt, ptk, kt) in enumerate(kts):
                                S.add("pe", lambda e, po=po, ii=ii, pt=pt, i=i, hh=hh, kt=kt, idx=idx, nk=len(kts): e.matmul(
                                    po[:, ii * 129:(ii + 1) * 129], pt[:, i * 128:(i + 1) * 128], vaug[:, hh, kt, 0:129],
                                    start=(idx == 0), stop=(idx == nk - 1)),
                                    reads=[ptk, ("vaug", hh, kt // 4), "vaug_ones"], writes=[pok])
                            if kts:
                                used.append((ii, i))
                        for ii, i in used:
                            S.add("dve", lambda e, po=po, ii=ii, i=i, n=n: e.scalar_tensor_tensor(
                                acc[:, i, 0:129], po[:, ii * 129:(ii + 1) * 129], sel[:, i, n:n + 1], acc[:, i, 0:129],
                                ALU.mult, ALU.add),
                                reads=[pok, ("sel", i), "acc"], writes=["acc"])
                if MST < 5:
                    return
                S.add("dve", lambda e: e.reciprocal(rcp[:, :], acc[:, :, 128]), reads=["acc"], writes=["rcp"])
                ys, ysk = ystg.next()
                for i in range(4):
                    S.add("dve", lambda e, i=i, ys=ys: e.tensor_scalar(ys[:, i * 128:(i + 1) * 128], acc[:, i, 0:128],
                                                                      rcp[:, i:i + 1], None, ALU.mult),
                          reads=["acc", "rcp"], writes=[ysk])
                emit_ytok(ys, ysk, 0 + hh)
            def unit_B(j, gi=gi, t0=t0):
                pc = PP_RG + 8 * j
                pgate, pgk = proj_fm(6 + j)
                g0, g0k = fnext()
                S.add("act", lambda e, g0=g0, pgate=pgate: e.copy(g0[:, :], pgate[:, :]), reads=[pgk], writes=[g0k])
                g1, g1k = fnext()
                S.add("act", lambda e, g1=g1, g0=g0: e.activation(g1[:, :], g0[:, :], AF.Square), reads=[g0k], writes=[g1k])
                S.add("dve", lambda e, g1=g1: e.tensor_scalar(g1[:, :], g1[:, :], 0.044715, 1.0, ALU.mult, ALU.add),
                      reads=[g1k], writes=[g1k])
                S.add("dve", lambda e, g1=g1, g0=g0: e.tensor_tensor(g1[:, :], g1[:, :], g0[:, :], ALU.mult),
                      reads=[g1k, g0k], writes=[g1k])
                S.add("act", lambda e, g1=g1: e.activation(g1[:, :], g1[:, :], AF.Sigmoid, scale=1.5957691216057308),
                      reads=[g1k], writes=[g1k])
                S.add("dve", lambda e, g1=g1, g0=g0: e.tensor_tensor(g1[:, :], g1[:, :], g0[:, :], ALU.mult),
                      reads=[g1k, g0k], writes=[g1k])
                px, pxk = proj_fm(8 + j)
                S.add("act", lambda e, px=px, j=j: e.copy(rgx[:, j, 3:3 + GT], px[:, :]), reads=[pxk], writes=[("rgx", j)])
                xc, xck = fnext()
                S.add("dve", lambda e, xc=xc, j=j, pc=pc: e.tensor_scalar(
                    xc[:, :], rgx[:, j, 0:GT], pp[:, pc:pc + 1], pp[:, pc + 4:pc + 5], ALU.mult, ALU.add),
                    reads=[("rgx", j), "pp"], writes=[xck])
                for k in range(1, 4):
                    S.add("dve", lambda e, xc=xc, j=j, pc=pc, k=k: e.scalar_tensor_tensor(
                        xc[:, :], rgx[:, j, k:k + GT], pp[:, pc + k:pc + k + 1], xc[:, :], ALU.mult, ALU.add),
                        reads=[("rgx", j), "pp", xck], writes=[xck])
                S.add("act", lambda e, j=j: e.copy(rgx[:, j, 0:3], rgx[:, j, GT:GT + 3]), reads=[("rgx", j), xck],
                      writes=[("rgx", j)])
                xcb, xcbk = bring.next()
                S.add("act", lambda e, xcb=xcb, xc=xc: e.copy(xcb[:, :], xc[:, :]), reads=[xck], writes=[xcbk])
                pr, prk = psn(pm)
                S.add("pe", lambda e, pr=pr, j=j, xcb=xcb: e.matmul(pr[:, :], rgw[:, j, :], xcb[:, :], start=True, stop=True),
                      reads=["rgw", xcbk], writes=[prk])
                pi, pik = psn(pm)
                S.add("pe", lambda e, pi=pi, j=j, xcb=xcb: e.matmul(pi[:, :], rgw[:, 2 + j, :], xcb[:, :], start=True, stop=True),
                      reads=["rgw", xcbk], writes=[pik])
                r, rk = fnext()
                S.add("act", lambda e, r=r, pr=pr, pc=pc: e.activation(r[:, :], pr[:, :], AF.Sigmoid, bias=pp[:, pc + 5:pc + 6], scale=1.0),
                      reads=[prk, "pp"], writes=[rk])
                ig, igk = fnext()
                S.add("act", lambda e, ig=ig, pi=pi, pc=pc: e.activation(ig[:, :], pi[:, :], AF.Sigmoid, bias=pp[:, pc + 6:pc + 7], scale=1.0),
                      reads=[pik, "pp"], writes=[igk])
                a, ak = fnext()
                S.add("act", lambda e, a=a, r=r, j=j: e.activation(a[:, :], r[:, :], AF.Exp, scale=der[:, j:j + 1]),
                      reads=[rk, "der"], writes=[ak])
                S.add("act", lambda e, r=r, j=j: e.activation(r[:, :], r[:, :], AF.Exp, scale=der[:, 2 + j:3 + j]),
                      reads=[rk, "der"], writes=[rk])
                S.add("dve", lambda e, r=r: e.tensor_scalar(r[:, :], r[:, :], -1.0, 1.0, ALU.mult, ALU.add),
                      reads=[rk], writes=[rk])
                S.add("dve", lambda e, r=r: e.tensor_scalar(r[:, :], r[:, :], 1e-30, None, ALU.max), reads=[rk], writes=[rk])
                S.add("act", lambda e, r=r: e.activation(r[:, :], r[:, :], AF.Sqrt), reads=[rk], writes=[rk])
                S.add("dve", lambda e, ig=ig, xc=xc: e.tensor_tensor(ig[:, :], ig[:, :], xc[:, :], ALU.mult),
                      reads=[igk, xck], writes=[igk])
                S.add("dve", lambda e, ig=ig, r=r: e.tensor_tensor(ig[:, :], ig[:, :], r[:, :], ALU.mult),
                      reads=[igk, rk], writes=[igk])
                h, hk = fnext()
                S.add("dve", lambda e, h=h, a=a, ig=ig, j=j: e.tensor_tensor_scan(
                    h[:, :], a[:, :], ig[:, :], rgh[:, j, 0:1], ALU.mult, ALU.add),
                    reads=[ak, igk, ("rgh", j)], writes=[hk])
                S.add("act", lambda e, h=h, j=j: e.copy(rgh[:, j, 0:1], h[:, GT - 1:GT]), reads=[hk], writes=[("rgh", j)])
                yb_, ybk_ = ybst.next()
                S.add("dve", lambda e, h=h, g1=g1, yb_=yb_: e.tensor_tensor(yb_[:, :], h[:, :], g1[:, :], ALU.mult),
                      reads=[hk, g1k], writes=[ybk_])
                emit_yfm(yb_, ybk_, 4 + j)
            def unit_C(j, gi=gi, t0=t0):
                pc = PP_CV + 3 * j
                pgt, pgtk = proj_fm(12 + j)
                sg, sgk = fnext()
                S.add("act", lambda e, sg=sg, pgt=pgt: e.activation(sg[:, :], pgt[:, :], AF.Sigmoid), reads=[pgtk], writes=[sgk])
                pval, pvk = proj_fm(10 + j)
                S.add("dve", lambda e, sg=sg, pval=pval, j=j: e.tensor_tensor(cvx[:, j, 30:30 + GT], sg[:, :], pval[:, :], ALU.mult),
                      reads=[sgk, pvk], writes=[("cvx", j)])
                pcv, pcvk = psn(pm)
                for k in range(31):
                    S.add("pe", lambda e, pcv=pcv, j=j, k=k: e.matmul(pcv[:, :], cvd[:, j, k, :], cvx[:, j, k:k + GT],
                                                                      start=(k == 0), stop=(k == 30)),
                          reads=["cvd", ("cvx", j)], writes=[pcvk])
                S.add("act", lambda e, j=j: e.copy(cvx[:, j, 0:30], cvx[:, j, GT:GT + 30]), reads=[("cvx", j), pcvk],
                      writes=[("cvx", j)])
                u, uk = fnext()
                S.add("act", lambda e, u=u, pcv=pcv, pc=pc: e.activation(u[:, :], pcv[:, :], AF.Identity, bias=pp[:, pc:pc + 1], scale=1.0),
                      reads=[pcvk, "pp"], writes=[uk])
                u2, u2k = fnext()
                S.add("act", lambda e, u=u, u2=u2: e.activation(u2[:, :], u[:, :], AF.Square), reads=[uk], writes=[u2k])
                pmn, pmnk = psn(pm)
                S.add("pe", lambda e, pmn=pmn, u=u: e.matmul(pmn[:, :], ones_f[:, :], u[:, :], start=True, stop=True),
                      reads=["ones_f", uk], writes=[pmnk])
                mn, mnk = fnext()
                S.add("act", lambda e, mn=mn, pmn=pmn: e.copy(mn[:, :], pmn[:, :]), reads=[pmnk], writes=[mnk])
                pm2, pm2k = psn(pm)
                S.add("pe", lambda e, pm2=pm2, u2=u2: e.matmul(pm2[:, :], ones_f[:, :], u2[:, :], start=True, stop=True),
                      reads=["ones_f", u2k], writes=[pm2k])
                S.add("dve", lambda e, u2=u2, mn=mn: e.tensor_tensor(u2[:, :], mn[:, :], mn[:, :], ALU.mult),
                      reads=[mnk, u2k], writes=[u2k])
                S.add("dve", lambda e, u2=u2, pm2=pm2: e.tensor_tensor(u2[:, :], pm2[:, :], u2[:, :], ALU.subtract),
                      reads=[pm2k, u2k], writes=[u2k])
                S.add("dve", lambda e, u2=u2: e.tensor_scalar(u2[:, :], u2[:, :], 0.0, None, ALU.max), reads=[u2k], writes=[u2k])
                S.add("act", lambda e, u2=u2: e.activation(u2[:, :], u2[:, :], AF.Ln, bias=der[:, 10:11], scale=1.0),
                      reads=[u2k, "der_c"], writes=[u2k])
                S.add("act", lambda e, u2=u2: e.activation(u2[:, :], u2[:, :], AF.Exp, scale=-0.5), reads=[u2k], writes=[u2k])
                S.add("dve", lambda e, u=u, mn=mn: e.tensor_tensor(u[:, :], u[:, :], mn[:, :], ALU.subtract),
                      reads=[uk, mnk], writes=[uk])
                S.add("dve", lambda e, u=u, u2=u2: e.tensor_tensor(u[:, :], u[:, :], u2[:, :], ALU.mult),
                      reads=[uk, u2k], writes=[uk])
                yb_, ybk_ = ybst.next()
                S.add("act", lambda e, u=u, pc=pc, yb_=yb_: e.activation(yb_[:, :], u[:, :], AF.Silu, bias=pp[:, pc + 2:pc + 3],
                                                                         scale=pp[:, pc + 1:pc + 2]),
                      reads=[uk, "pp"], writes=[ybk_])
                emit_yfm(yb_, ybk_, 6 + j)
            def unit_D(hh, gi=gi, t0=t0):
                lb, oml, noml = der[:, 4 + hh:5 + hh], der[:, 6 + hh:7 + hh], der[:, 8 + hh:9 + hh]
                pq, pqk = proj_fm(14 + hh)
                qf, qfk = fnext()
                S.add("act", lambda e, qf=qf, pq=pq: e.copy(qf[:, :], pq[:, :]), reads=[pqk], writes=[qfk])
                pf, pfk = proj_fm(16 + hh)
                sg, sgk = fnext()
                S.add("act", lambda e, sg=sg, pf=pf: e.activation(sg[:, :], pf[:, :], AF.Sigmoid), reads=[pfk], writes=[sgk])
                lf, lfk = fnext()
                S.add("dve", lambda e, lf=lf, sg=sg, oml=oml, lb=lb: e.tensor_scalar(lf[:, :], sg[:, :], oml, lb, ALU.mult, ALU.add),
                      reads=[sgk, "der"], writes=[lfk])
                S.add("act", lambda e, lf=lf: e.activation(lf[:, :], lf[:, :], AF.Ln), reads=[lfk], writes=[lfk])
                S.add("dve", lambda e, sg=sg, oml=oml, noml=noml: e.tensor_scalar(sg[:, :], sg[:, :], noml, oml, ALU.mult, ALU.add),
                      reads=[sgk, "der"], writes=[sgk])
                bcs, bck = fnext()
                S.add("dve", lambda e, bcs=bcs, lf=lf: e.tensor_tensor_scan(bcs[:, :], cmask, lf[:, :], 0.0, ALU.mult, ALU.add),
                      reads=[lfk, "cst"], writes=[bck])
                eb, ebk = fnext()
                S.add("act", lambda e, eb=eb, bcs=bcs: e.activation(eb[:, :], bcs[:, :], AF.Exp), reads=[bck], writes=[ebk])
                S.add("dve", lambda e, lf=lf, bcs=bcs: e.tensor_tensor(
                    lf[:, :].rearrange("p (c k) -> p c k", k=HC),
                    bcs[:, :].rearrange("p (c k) -> p c k", k=HC)[:, :, HC - 1:HC].broadcast_to([128, GT // HC, HC]),
                    bcs[:, :].rearrange("p (c k) -> p c k", k=HC), ALU.subtract),
                    reads=[bck, lfk], writes=[lfk])
                S.add("act", lambda e, lf=lf: e.activation(lf[:, :], lf[:, :], AF.Exp), reads=[lfk], writes=[lfk])
                S.add("act", lambda e, bcs=bcs: e.activation(bcs[:, :], bcs[:, :], AF.Exp, scale=-1.0), reads=[bck], writes=[bck])
                qd, qdk = bring.next()
                S.add("dve", lambda e, qd=qd, qf=qf, eb=eb: e.tensor_tensor(qd[:, :], qf[:, :], eb[:, :], ALU.mult),
                      reads=[qfk, ebk], writes=[qdk])
                kd, kdk = bring.next()
                S.add("dve", lambda e, kd=kd, sg=sg, bcs=bcs: e.tensor_tensor(kd[:, :], sg[:, :], bcs[:, :], ALU.mult),
                      reads=[sgk, bck], writes=[kdk])
                kh, khk = bring.next()
                S.add("dve", lambda e, kh=kh, sg=sg, lf=lf: e.tensor_tensor(kh[:, :], sg[:, :], lf[:, :], ALU.mult),
                      reads=[sgk, lfk], writes=[khk])
                pv, pvk = proj_tm(18 + hh)
                S.add("act", lambda e, pv=pv: e.copy(vtok[:, :, :], pv[:, :].rearrange("p (t c) -> p t c", t=4)),
                      reads=[pvk], writes=["vtok"])
                pgg, pggk = proj_tm(20 + hh)
                S.add("act", lambda e, pgg=pgg: e.activation(gtok[:, :, :], pgg[:, :].rearrange("p (t c) -> p t c", t=4), AF.Silu),
                      reads=[pggk], writes=["gtok"])
                ys, ysk = ystg.next()
                for tt in range(4):
                    c0 = tt * 128
                    ptr, ptrk = psn(pm)
                    ptr_b = ptr[:, :].bitcast(BF16)
                    S.add("pe", lambda e, ptr_b=ptr_b, kh=kh, c0=c0: e.transpose(ptr_b[:, 0:128], kh[:, c0:c0 + 128], identb[:, :]),
                          reads=[khk, "identb"], writes=[ptrk])
                    for c in range(4):
                        S.add("dve", lambda e, ptr_b=ptr_b, c=c: e.tensor_scalar(
                            khz[:, c, :], ptr_b[:, 0:128], cst[:, C_RM + c:C_RM + c + 1], None, ALU.mult),
                            reads=[ptrk, "cst"], writes=[("khz", c)])
                    for c in range(4):
                        S.add("act", lambda e, qd=qd, c=c, c0=c0: e.copy(qdz[:, c, c * HC:(c + 1) * HC],
                                                                       qd[:, c0 + c * HC:c0 + (c + 1) * HC]),
                              reads=[qdk], writes=[("qdz", c)])
                    pat, patk = psn(pm)
                    S.add("pe", lambda e, pat=pat, kd=kd, qd=qd, c0=c0: e.matmul(
                        pat[:, 0:128], kd[:, c0:c0 + 128], qd[:, c0:c0 + 128], start=True, stop=True),
                        reads=[kdk, qdk], writes=[patk])
                    S.add("dve", lambda e, pat=pat: e.tensor_tensor(attm[:, :], pat[:, 0:128], bdm[:, :], ALU.mult),
                          reads=[patk, "bdm"], writes=["attm"])
                    po, pok = psn(pa)
                    S.add("pe", lambda e, po=po, tt=tt: e.matmul(po[:, 0:128], attm[:, :], vtok[:, tt, :], start=True, stop=False),
                          reads=["attm", "vtok"], writes=[pok])
                    for c in range(4):
                        S.add("pe", lambda e, po=po, c=c, hh=hh: e.matmul(po[:, 0:128], qdz[:, c, :], Sb[:, hh, :],
                                                                        start=False, stop=(c == 3)),
                              reads=[("qdz", c), ("Sb", hh)], writes=[pok])
                        pst, pstk = psn(pm)
                        S.add("pe", lambda e, pst=pst, c=c, tt=tt: e.matmul(pst[:, 0:128], khz[:, c, :], vtok[:, tt, :],
                                                                          start=True, stop=True),
                              reads=[("khz", c), "vtok"], writes=[pstk])
                        col = c0 + (c + 1) * HC - 1
                        S.add("dve", lambda e, pst=pst, hh=hh, eb=eb, col=col: e.scalar_tensor_tensor(
                            Sb[:, hh, :], S32[:, hh, :], eb[:, col:col + 1], pst[:, 0:128], ALU.mult, ALU.add),
                            reads=[pstk, ebk, ("S32", hh)], writes=[("Sb", hh)])
                        S.add("dve", lambda e, pst=pst, hh=hh, eb=eb, col=col: e.scalar_tensor_tensor(
                            S32[:, hh, :], S32[:, hh, :], eb[:, col:col + 1], pst[:, 0:128], ALU.mult, ALU.add),
                            reads=[pstk, ebk, ("S32", hh)], writes=[("S32", hh)])
                    sq, sqk = fnext()
                    S.add("act", lambda e, sq=sq, po=po, tt=tt: e.activation(sq[:, 0:128], po[:, 0:128], AF.Square,
                                                                           accum_out=hsm[:, tt:tt + 1]),
                          reads=[pok], writes=[sqk, ("hsm", tt)])
                    S.add("dve", lambda e, tt=tt: e.tensor_scalar(hsm[:, tt:tt + 1], hsm[:, tt:tt + 1], 1.0 / 128.0, LN_EPS, ALU.mult, ALU.add),
                          reads=[("hsm", tt)], writes=[("hsm", tt)])
                    S.add("act", lambda e, tt=tt: e.activation(hsm[:, tt:tt + 1], hsm[:, tt:tt + 1], AF.Sqrt),
                          reads=[("hsm", tt)], writes=[("hsm", tt)])
                    S.add("dve", lambda e, tt=tt: e.reciprocal(hsm[:, 4 + tt:5 + tt], hsm[:, tt:tt + 1]),
                          reads=[("hsm", tt)], writes=[("hsmr", tt)])
                    S.add("dve", lambda e, sq=sq, po=po, tt=tt, hh=hh: e.scalar_tensor_tensor(
                        sq[:, 0:128], po[:, 0:128], hsm[:, 4 + tt:5 + tt], hgn[:, hh * 128:(hh + 1) * 128], ALU.mult, ALU.mult),
                        reads=[pok, ("hsmr", tt), "hgn", sqk], writes=[sqk])
                    S.add("dve", lambda e, sq=sq, ys=ys, tt=tt: e.tensor_tensor(ys[:, tt * 128:(tt + 1) * 128], sq[:, 0:128],
                                                                              gtok[:, tt, :], ALU.mult),
                          reads=[sqk, "gtok"], writes=[ysk])
                emit_ytok(ys, ysk, 2 + hh)
            bg = cfg.get("bg") or []
            nbg = cfg.get("bg_per_unit", 0)

            def emit_bg():
                for _ in range(nbg):
                    if bg:
                        bg.pop(0)()
            S.stream("X")
            cur["s"] = "X"
            for u_ in range(2):
                if "A" in parts:
                    emit_bg()
                    unit_A(u_)
            for u_ in range(2):
                if "B" in parts:
                    emit_bg()
                    unit_B(u_)
            S.stream("Y")
            cur["s"] = "Y"
            for u_ in range(2):
                if "C" in parts:
                    emit_bg()
                    unit_C(u_)
            for u_ in range(2):
                if "D" in parts:
                    emit_bg()
                    unit_D(u_)
            S.merge_streams()
            if gi == ng - 1:
                while bg:
                    bg.pop(0)()
            cur["s"] = "X"
            def emit_cc(g_):
                ccg = cfg["cc"](g_)
                if ccg is not None:
                    csrc, cdst, ckey = ccg
                    S.cc("pool", lambda e, csrc=csrc, cdst=cdst: e.collective_compute(
                        "AllGather", ALU.bypass, replica_groups=PAIRS, ins=[csrc], outs=[cdst]),
                        reads=[("dram", "YS", g_)], writes=[ckey], inc=1)
            if gi > 0:
                emit_cc(gi - 1)
            if gi == ng - 1:
                emit_cc(gi)


def mixer_inputs(x_seq, g, l, w_in, rg_conv_w, rg_conv_b, rg_w_a, rg_b_a, rg_w_x, rg_b_x, rg_lambda,
                 cv_w, cv_b, cv_ln_g, cv_ln_b, hg_lower_bounds, hg_norm_g, consts):
    cols = []
    for s in range(11):
        for u in range(2):
            c0 = s * 512 + (g * 2 + u) * 128
            cols.append(np.arange(c0, c0 + 128))
    cols = np.concatenate(cols)
    win = np.ascontiguousarray(w_in[l][:, cols])
    pp = np.zeros((128, NPP), np.float32)
    for j in range(2):
        ch = g * 256 + j * 128 + np.arange(128)
        for k in range(4):
            pp[:, PP_RG + 8 * j + k] = rg_conv_w[l, k, ch]
        pp[:, PP_RG + 8 * j + 4] = rg_conv_b[l, ch]
        pp[:, PP_RG + 8 * j + 5] = rg_b_a[l, ch]
        pp[:, PP_RG + 8 * j + 6] = rg_b_x[l, ch]
        pp[:, PP_RG + 8 * j + 7] = rg_lambda[l, ch]
        pp[:, PP_CV + 3 * j] = cv_b[l, ch]
        pp[:, PP_CV + 3 * j + 1] = cv_ln_g[l, ch]
        pp[:, PP_CV + 3 * j + 2] = cv_ln_b[l, ch]
        for k in range(31):
            pp[:, PP_CVW + 31 * j + k] = cv_w[l, k, ch]
        pp[:, PP_HG + 3 * j] = hg_lower_bounds[0, ch]
        pp[:, PP_HG + 3 * j + 1] = hg_lower_bounds[1, ch]
        pp[:, PP_HG + 3 * j + 2] = float(l == 1)
    rgw = np.zeros((128, 4, 128), np.float32)
    for j in range(2):
        rgw[:, j, :] = rg_w_a[l, g * 2 + j]
        rgw[:, 2 + j, :] = rg_w_x[l, g * 2 + j]
    hgn = np.ascontiguousarray(hg_norm_g[l, g * 256:(g + 1) * 256]).reshape(1, 256)
    return {"xT": None if x_seq is None else np.ascontiguousarray(x_seq.T), "win": win, "pp": pp,
            "consts": consts, "rgw": rgw.reshape(128, 512), "hgn": hgn}


HALF = SEQ // 2
ARENA_COLS = 207 * 512


def wout_row_perm(g_list=(0, 1)):
    rows = []
    for g in g_list:
        for base, n in ((0, 2), (1536, 2), (512, 2), (1024, 2)):
            for u in range(n):
                r0 = base + (g * 2 + u) * 128
                rows.append(np.arange(r0, r0 + 128))
    return np.concatenate(rows)


def build_fused():
    nc = bass.Bass("TRN2", target_bir_lowering=False)
    din = lambda name, shape: nc.dram_tensor(name, shape, F32, kind="ExternalInput").ap()
    x_tok = din("x_tok", [HALF, D_MODEL])
    x_T = din("x_T", [D_MODEL, HALF])
    consts = din("consts", [128, NCONST])
    flags = din("flags", [128, 2])
    W = []
    for l in range(DEPTH):
        d = {}
        for i in range(2):
            d["wg%d" % i] = din("wg%d%d" % (l, i), [D_MODEL, D_FF])
            d["wu%d" % i] = din("wu%d%d" % (l, i), [D_MODEL, D_FF])
            d["wd%d" % i] = din("wd%d%d" % (l, i), [D_FF, D_MODEL])
        for i in range(3):
            d["lng%d" % i] = din("lng%d%d" % (l, i), [1, D_MODEL])
            d["lnb%d" % i] = din("lnb%d%d" % (l, i), [1, D_MODEL])
        d["wo"] = din("wo%d" % l, [D_MODEL, D_MODEL])
        d["win"] = din("win%d" % l, [D_MODEL, 2816])
        d["pp"] = din("pp%d" % l, [128, NPP])
        d["rgw"] = din("rgw%d" % l, [128, 512])
        d["hgn"] = din("hgn%d" % l, [1, 256])
        W.append(d)
    out = nc.dram_tensor("out", [HALF, D_MODEL], F32, kind="ExternalOutput").ap()
    scr = lambda name, shape, dt: nc.dram_tensor(name, shape, dt, kind="Internal").ap()
    R = [scr("xres%d" % i, [HALF, D_MODEL], F32) for i in range(2)]
    T = [[scr("xTb%d_%d" % (s, b), [D_MODEL, TB], BF16) for b in range(4)] for s in range(2)]
    XG = [scr("xg%d" % b, [2 * D_MODEL, TB], BF16) for b in range(4)]
    YS = [scr("ysrc%d" % g, [1024, GT], BF16) for g in range(8)]
    YG = [scr("yg%d" % g, [2048, GT], BF16) for g in range(8)]
    WSS = [[scr("ws%d_%d" % (k_, s_), [128, 8192], BF16) for s_ in range(34)] for k_ in range(3)]
    WSO = [scr("wso%d" % s_, [128, 8192], BF16) for s_ in range(4)]

    def precast_list(S, wg_, wu_, wd_, ws_):
        wg_v = wg_.rearrange("(kt p) c -> p kt c", p=128)
        wu_v = wu_.rearrange("(kt p) c -> p kt c", p=128)
        wd_v = wd_.rearrange("(kt p) c -> p kt c", p=128)
        lst = []
        for cp in range(22):
            for m, wv_ in enumerate((wg_v, wu_v)):
                lst.append(lambda cp=cp, m=m, wv_=wv_: S.dma("pool", lambda e: e.dma_start(
                    out=ws_[cp][:, m * 4096:(m + 1) * 4096].rearrange("p (kt c) -> p kt c", kt=16),
                    in_=wv_[:, :, cp * 256:(cp + 1) * 256]), writes=[("dram", "WSbg", id(ws_), cp, m)], slotq="poolbg"))
        for n in range(4):
            for ch, (k0, k1) in enumerate(((0, 16), (16, 32), (32, 44))):
                si = 22 + n * 3 + ch
                nk = k1 - k0
                lst.append(lambda si=si, nk=nk, k0=k0, k1=k1, n=n: S.dma("pool", lambda e: e.dma_start(
                    out=ws_[si][:, :nk * 512].rearrange("p (kt c) -> p kt c", kt=nk),
                    in_=wd_v[:, k0:k1, n * 512:(n + 1) * 512]), writes=[("dram", "WSbg", id(ws_), si)], slotq="poolbg"))
        return lst

    with ExitStack() as st:
        arena_t = st.enter_context(nc.sbuf_tensor("arena", [128, ARENA_COLS], BF16))
        sb = Arena(arena_t, ARENA_COLS)
        ps = [st.enter_context(nc.psum_tensor("ps%d" % i, [128, 512], F32)) for i in range(8)]
        S = Sched(nc)
        out_ops = []
        first = [True]

        def phase():
            if not first[0]:
                S.barrier()
            first[0] = False
            sb.reset()

        cur = None
        bg_store = {}
        for l in range(DEPTH):
            w = W[l]
            for i in range(2):
                phase()
                last = (l == DEPTH - 1 and i == 1)
                if cur is None:
                    xres_in, nxt = x_tok, 0
                    xT_v = x_T.rearrange("(kt p) t -> p kt t", p=128)
                    xT_src = lambda bt, xT_v=xT_v: ("pool", xT_v[:, :, bt * TB:(bt + 1) * TB], ("dram", "x_T"))
                else:
                    xres_in, nxt = R[cur], 1 - cur
                    xT_src = lambda bt, s=cur: ("sp", T[s][bt].rearrange("(kt p) t -> p kt t", p=128), ("dram", "T", s, bt))
                if last:
                    xo = lambda bt: None
                    out_res = out
                else:
                    out_res = R[nxt]
                    if i == 0:
                        xo = lambda bt, s=nxt: {"xTb": T[s][bt], "key": ("dram", "T", s, bt),
                                                "cc": (T[s][bt], XG[bt], ("dram", "XG", bt))}
                    else:
                        xo = lambda bt, s=nxt: {"xTb": T[s][bt], "key": ("dram", "T", s, bt), "cc": None}
                emit_ffn(S, sb, ps, {"ntok": HALF, "xres_in": xres_in, "xT_src": xT_src, "wg": w["wg%d" % i],
                                     "wu": w["wu%d" % i], "wd": w["wd%d" % i], "lng": w["lng%d" % (2 * i)],
                                     "lnb": w["lnb%d" % (2 * i)], "out_res": out_res, "xo": xo, "consts": consts,
                                     "out_ops": out_ops,
                                     "ws": WSS[1] if i == 1 else WSS[2 * (l % 2)],
                                     "ws_ready": (i == 1),
                                     "bg": bg_store.pop("ffn2", []) if i == 1 else []})
                cur = nxt
                if i == 1:
                    break
                phase()
                bg = precast_list(S, w["wg1"], w["wu1"], w["wd1"], WSS[1])
                bg_store["wout"] = []
                bg_store["ffn2"] = []
                wo_v_ = w["wo"].rearrange("(kt p) c -> p kt c", p=128)
                for n_ in range(4):
                    bg.append(lambda n_=n_, wo_v_=wo_v_: S.dma("pool", lambda e: e.dma_start(
                        out=WSO[n_].rearrange("p (kt c) -> p kt c", kt=16), in_=wo_v_[:, :, n_ * 512:(n_ + 1) * 512]),
                        writes=[("dram", "WSO", n_)], slotq="poolbg"))
                emit_mixer(S, sb, ps, {
                    "bg": bg, "bg_per_unit": (len(bg) + 63) // 64,
                    "seq": SEQ,
                    "xg": lambda gi: (XG[gi % 4].rearrange("(r kt p) t -> r p kt t", r=2, p=128)[gi // 4], ("dram", "XG", gi % 4)),
                    "win": w["win"], "pp": w["pp"], "consts": consts, "rgw": w["rgw"], "hgn": w["hgn"],
                    "ysrc": lambda gi: (YS[gi], ("dram", "YS", gi)),
                    "cc": lambda gi: (YS[gi], YG[gi], ("dram", "YG", gi)),
                    "out_ops": out_ops})
                phase()
                nxt = 1 - cur
                emit_wout(S, sb, ps, {"ntok": HALF, "xres_in": R[cur], "yg": lambda k: (YG[k], ("dram", "YG", k)),
                                      "wo": w["wo"], "lng": w["lng1"], "lnb": w["lnb1"], "out_res": R[nxt],
                                      "xo": lambda bt, s=nxt: {"xTb": T[s][bt], "key": ("dram", "T", s, bt), "cc": None},
                                      "consts": consts, "flags": flags, "out_ops": out_ops,
                                      "bg": bg_store.get("wout", []), "wso": WSO})
                cur = nxt
        S.barrier()
        S.wait_all("sp", out_ops)
        S.emit(st)
        print("fused instr counts", S.n_instr)
    return nc


_PROG = []


def kernel(x, ln_g, ln_b, ffn_w_gate, ffn_w_up, ffn_w_down, w_in, w_out, rg_conv_w, rg_conv_b, rg_w_a,
           rg_b_a, rg_w_x, rg_b_x, rg_lambda, cv_w, cv_b, cv_ln_g, cv_ln_b, hg_lower_bounds, hg_norm_g):
    f = lambda a: np.ascontiguousarray(np.asarray(a, dtype=np.float32))
    x = f(x)
    (ln_g, ln_b, ffn_w_gate, ffn_w_up, ffn_w_down, w_in, w_out, rg_conv_w, rg_conv_b, rg_w_a, rg_b_a, rg_w_x,
     rg_b_x, rg_lambda, cv_w, cv_b, cv_ln_g, cv_ln_b, hg_lower_bounds, hg_norm_g) = [f(a) for a in (
         ln_g, ln_b, ffn_w_gate, ffn_w_up, ffn_w_down, w_in, w_out, rg_conv_w, rg_conv_b, rg_w_a, rg_b_a, rg_w_x,
         rg_b_x, rg_lambda, cv_w, cv_b, cv_ln_g, cv_ln_b, hg_lower_bounds, hg_norm_g)]
    if not _PROG:
        _PROG.append(build_fused())
    nc = _PROG[0]
    consts = make_consts()
    perm = wout_row_perm()
    shared = {"consts": consts}
    for l in range(DEPTH):
        for i in range(2):
            shared["wg%d%d" % (l, i)] = ffn_w_gate[l, i]
            shared["wu%d%d" % (l, i)] = ffn_w_up[l, i]
            shared["wd%d%d" % (l, i)] = ffn_w_down[l, i]
        for i in range(3):
            shared["lng%d%d" % (l, i)] = ln_g[l, i].reshape(1, -1)
            shared["lnb%d%d" % (l, i)] = ln_b[l, i].reshape(1, -1)
        shared["wo%d" % l] = f(w_out[l][perm])
    in_maps = []
    for c in range(N_CORES):
        b, r = c // 2, c % 2
        xs_ = x[b, r * HALF:(r + 1) * HALF]
        m = dict(shared)
        m["x_tok"] = f(xs_)
        m["x_T"] = f(xs_.T)
        fl = np.zeros((128, 2), np.float32)
        fl[:, r] = 1.0
        m["flags"] = fl
        for l in range(DEPTH):
            mi = mixer_inputs(None, r, l, w_in, rg_conv_w, rg_conv_b, rg_w_a, rg_b_a, rg_w_x, rg_b_x, rg_lambda,
                              cv_w, cv_b, cv_ln_g, cv_ln_b, hg_lower_bounds, hg_norm_g, consts)
            m["win%d" % l] = mi["win"]
            m["pp%d" % l] = mi["pp"]
            m["rgw%d" % l] = mi["rgw"]
            m["hgn%d" % l] = mi["hgn"]
        in_maps.append(m)
    res = run_bass_kernel_spmd(nc, in_maps, core_ids=list(range(N_CORES)))
    out = np.empty((BATCH, SEQ, D_MODEL), np.float32)
    for c in range(N_CORES):
        out[c // 2, (c % 2) * HALF:(c % 2 + 1) * HALF] = res.results[c]["out"]
    return out
```

```python
from contextlib import ExitStack

import os
import numpy as np
import concourse.bass as bass
import concourse.mybir as mybir
from concourse.bass_utils import run_bass_kernel_spmd

F32 = mybir.dt.float32
BF16 = mybir.dt.bfloat16
AF = mybir.ActivationFunctionType
ALU = mybir.AluOpType
AX = mybir.AxisListType

D_MODEL = 2048
BATCH = 4
SEQ = 4096
DEPTH = 2
D_GROUP = 512
D_FF = 5632
D_IN = 11 * D_GROUP
LN_EPS = 1e-5
ALPHA = (2 * DEPTH) ** 0.25
N_CORES = 8

DMA_SLOTS = 8


class _Op:
    __slots__ = ("eng", "fn", "reads", "writes", "is_dma", "deps", "signal", "semval",
                 "slot", "waits", "idx", "cc_inc", "slotq")

    def __init__(self, eng, fn, reads, writes, is_dma):
        self.eng = eng
        self.fn = fn
        self.reads = tuple(reads)
        self.writes = tuple(writes)
        self.is_dma = is_dma
        self.deps = []
        self.signal = False
        self.semval = None
        self.slot = None
        self.waits = []
        self.cc_inc = None
        self.slotq = None


class Sched:
    ENGS = ("pe", "act", "dve", "pool", "sp")

    def __init__(self, nc):
        self.nc = nc
        self.ops = []
        self.last_w = {}
        self.readers = {}

    def add(self, eng, fn, reads=(), writes=(), is_dma=False):
        op = _Op(eng, fn, reads, writes, is_dma)
        cur = getattr(self, "_cur", None)
        if cur is not None:
            self._streams[cur].append(op)
        else:
            self._commit(op)
        return op

    def _commit(self, op):
        op.idx = len(self.ops)
        deps = set()
        for k in op.reads:
            w = self.last_w.get(k)
            if w is not None:
                deps.add(w)
        for k in op.writes:
            w = self.last_w.get(k)
            if w is not None:
                deps.add(w)
            for r in self.readers.get(k, ()):
                deps.add(r)
        deps.discard(op.idx)
        op.deps = sorted(deps)
        for k in op.reads:
            self.readers.setdefault(k, []).append(op.idx)
        for k in op.writes:
            self.last_w[k] = op.idx
            self.readers[k] = []
        self.ops.append(op)

    def stream(self, name):
        if not hasattr(self, "_streams"):
            self._streams = {}
        self._streams.setdefault(name, [])
        self._cur = name

    def merge_streams(self):
        lists = [v for v in self._streams.values() if v]
        self._cur = None
        self._streams = {}
        pos = [0] * len(lists)
        while True:
            best, bf = None, None
            for i, l in enumerate(lists):
                if pos[i] < len(l):
                    fr = pos[i] / len(l)
                    if bf is None or fr < bf:
                        best, bf = i, fr
            if best is None:
                break
            self._commit(lists[best][pos[best]])
            pos[best] += 1

    def dma(self, queue, fn, reads=(), writes=(), slotq=None):
        op = self.add(queue, fn, reads, writes, is_dma=True)
        op.slotq = slotq
        return op

    def cc(self, queue, fn, reads=(), writes=(), inc=16):
        op = self.add(queue, fn, reads, writes, is_dma=True)
        op.cc_inc = inc
        return op

    def barrier(self):
        start = getattr(self, "bar_start", 0)
        last = {}
        dmas = []
        carry_ops = getattr(self, "_carry", [])
        for op in self.ops[start:]:
            if op.is_dma:
                if op.cc_inc is not None:
                    carry_ops.append(op)
                else:
                    dmas.append(op)
            elif op.fn is not None:
                last[op.eng] = op
        deps = list(last.values()) + dmas
        for e in self.ENGS:
            self.wait_all(e, deps)
        keep = {}
        live = []
        for op in carry_ops:
            ks = [k for k in op.writes if self.last_w.get(k) == op.idx]
            for k in ks:
                keep[k] = op.idx
            if ks:
                live.append(op)
        self._carry = live
        self.last_w.clear()
        self.readers.clear()
        self.last_w.update(keep)
        self.bar_start = len(self.ops)

    def wait_all(self, eng, ops):
        op = _Op(eng, None, (), (), False)
        op.idx = len(self.ops)
        op.deps = sorted(o.idx for o in ops)
        self.ops.append(op)
        return op

    def emit(self, stack):
        nc = self.nc
        ops = self.ops
        dma_count = {}
        slot_hist = {}
        slot_val = {}
        for op in ops:
            if op.is_dma:
                qn = (op.slotq or op.eng) if op.cc_inc is None else "cc"
                n = dma_count.get(qn, 0)
                dma_count[qn] = n + 1
                op.slot = (qn, n % (DMA_SLOTS if op.cc_inc is None else 2))
                slot_val[op.slot] = slot_val.get(op.slot, 0) + (16 if op.cc_inc is None else op.cc_inc)
                op.semval = slot_val[op.slot]
                prev = slot_hist.get(op.slot)
                if prev is not None and prev not in op.deps:
                    op.deps.append(prev)
                slot_hist[op.slot] = op.idx
        for op in ops:
            for d in op.deps:
                p = ops[d]
                if p.is_dma:
                    continue
                if p.eng == "pe" and op.eng == "pe" and not op.is_dma:
                    continue
                p.signal = True
        cnt = {e: 0 for e in self.ENGS}
        for op in ops:
            if not op.is_dma and op.signal:
                cnt[op.eng] += 1
                op.semval = cnt[op.eng]
        sems = {}
        for e in self.ENGS:
            sems[("c", e)] = stack.enter_context(nc.semaphore("sem_" + e))
        for q in dma_count:
            for s in range(DMA_SLOTS):
                sems[("d", q, s)] = stack.enter_context(nc.semaphore("dsem_%s_%d" % (q, s)))
        seen = {e: {} for e in self.ENGS}
        per_eng = {e: [] for e in self.ENGS}
        for op in ops:
            for d in op.deps:
                p = ops[d]
                if p.is_dma:
                    key = ("d",) + p.slot
                else:
                    if p.eng == "pe" and op.eng == "pe" and not op.is_dma:
                        continue
                    key = ("c", p.eng)
                val = p.semval
                if seen[op.eng].get(key, 0) >= val:
                    continue
                seen[op.eng][key] = val
                op.waits.append((key, val))
            per_eng[op.eng].append(op)
        self.n_instr = {e: len(per_eng[e]) for e in self.ENGS}

        def run(eng_obj, lst, ename):
            for op in lst:
                for key, val in op.waits:
                    eng_obj.wait_ge(sems[key], val)
                if op.fn is None:
                    continue
                inst = op.fn(eng_obj)
                if op.is_dma:
                    inst.then_inc(sems[("d",) + op.slot], 16 if op.cc_inc is None else op.cc_inc)
                elif op.signal:
                    inst.then_inc(sems[("c", ename)], 1)

        block = stack.enter_context(nc.Block())

        @block.tensor
        def _(e):
            run(e, per_eng["pe"], "pe")

        @block.scalar
        def _(e):
            run(e, per_eng["act"], "act")

        @block.vector
        def _(e):
            run(e, per_eng["dve"], "dve")

        @block.gpsimd
        def _(e):
            run(e, per_eng["pool"], "pool")

        @block.sync
        def _(e):
            run(e, per_eng["sp"], "sp")


class Ring:
    def __init__(self, name, tiles):
        self.name = name
        self.tiles = tiles
        self.i = 0

    def next(self):
        s = self.i % len(self.tiles)
        self.i += 1
        return self.tiles[s], (self.name, s)


class Arena:
    def __init__(self, t, ncols):
        self.t = t
        self.ncols = ncols
        self.off = 0

    def reset(self):
        self.off = 0

    def __call__(self, name, shape, dt=F32):
        n = 1
        for s in shape[1:]:
            n *= s
        w = n * (2 if dt == F32 else 1)
        w = (w + 15) // 16 * 16
        assert self.off + w <= self.ncols, ("SBUF arena overflow", name, self.off, w)
        v = self.t[:, self.off:self.off + (n * 2 if dt == F32 else n)]
        self.off += w
        if dt == F32:
            v = v.bitcast(F32)
        if len(shape) == 3:
            v = v.rearrange("p (a b) -> p a b", a=shape[1])
        elif len(shape) == 4:
            v = v.rearrange("p (a b c) -> p a b c", a=shape[1], b=shape[2])
        return v


TB = 512


def _ln_tail(S, z, ztag, st, lnw, out_ap, out_key, out_ops, tmpn, xt=None, oq="sp", g_eng="dve"):
    stats, mv, sd, nmr = st
    kst = ("lnstat", tmpn)
    for c in range(4):
        S.add("dve", lambda e, c=c: e.bn_stats(stats[:, c * 6:(c + 1) * 6], z[:, c * 512:(c + 1) * 512]),
              reads=[ztag], writes=[kst])
    S.add("dve", lambda e: e.bn_aggr(mv[:, :], stats[:, :]), reads=[kst], writes=[("lnmv", tmpn)])
    S.add("act", lambda e: e.activation(sd[:, 0:1], mv[:, 1:2], AF.Sqrt, bias=lnw["eps"][:, 0:1], scale=1.0),
          reads=[("lnmv", tmpn), "lnw"], writes=[("lnsd", tmpn)])
    S.add("dve", lambda e: e.reciprocal(sd[:, 1:2], sd[:, 0:1]), reads=[("lnsd", tmpn)], writes=[("lnrs", tmpn)])
    S.add("dve", lambda e: e.scalar_tensor_tensor(nmr[:, 0:1], mv[:, 0:1], -1.0, sd[:, 1:2], ALU.mult, ALU.mult),
          reads=[("lnmv", tmpn), ("lnrs", tmpn)], writes=[("lnnm", tmpn)])
    S.add("act", lambda e: e.activation(z[:, :], z[:, :], AF.Identity, bias=nmr[:, 0:1], scale=sd[:, 1:2]),
          reads=[ztag, ("lnrs", tmpn), ("lnnm", tmpn)], writes=[ztag])
    S.add(g_eng, lambda e: e.tensor_tensor(z[:, :], z[:, :], lnw["g"][:, :], ALU.mult),
          reads=[ztag, "lnw"], writes=[ztag])
    S.add("dve", lambda e: e.tensor_tensor(z[:, :], z[:, :], lnw["b"][:, :], ALU.add),
          reads=[ztag, "lnw"], writes=[ztag])
    o = S.dma(oq, lambda e: e.dma_start(out=out_ap, in_=z[:, :]), reads=[ztag], writes=[out_key])
    out_ops.append(o)
    if xt is not None:
        _ln_transposes(S, z, ztag, xt)


def _ln_transposes(S, z, ztag, xt):
    if True:
        ps, ident, stage, t = xt["ps"], xt["ident"], xt["stage"], xt["t"]
        banks = xt.get("banks", (0, 1, 2, 3))
        for q in range(4):
            bq = banks[q % len(banks)]
            for jj in range(4):
                kt = 4 * q + jj
                S.add("pe", lambda e, bq=bq, jj=jj, kt=kt: e.transpose(
                    ps[bq][:, jj * 128:(jj + 1) * 128], z[:, kt * 128:(kt + 1) * 128], ident),
                    reads=[ztag, "ident"], writes=[("ps", bq)])
            S.add("act", lambda e, q=q, bq=bq: e.copy(stage[:, 4 * q:4 * q + 4, t * 128:(t + 1) * 128],
                                                      ps[bq][:, :].rearrange("p (j c) -> p j c", j=4)),
                  reads=[("ps", bq)], writes=["xstage"])


def _alloc_ln(sb, nstat):
    lnw = {"g": sb("lng_s", [128, D_MODEL], F32), "b": sb("lnb_s", [128, D_MODEL], F32),
           "eps": sb("eps_s", [128, 1], F32)}
    stt = [(sb("stats%d" % i, [128, 24], F32), sb("mv%d" % i, [128, 2], F32),
            sb("sd%d" % i, [128, 2], F32), sb("nmr%d" % i, [128, 1], F32)) for i in range(nstat)]
    return lnw, stt


def _load_ln(S, lnw, lng, lnb):
    S.dma("sp", lambda e: e.dma_start(out=lnw["g"][:, :], in_=lng.partition_broadcast(128)), writes=["lnw"])
    S.dma("sp", lambda e: e.dma_start(out=lnw["b"][:, :], in_=lnb.partition_broadcast(128)), writes=["lnw"])
    S.add("dve", lambda e: e.memset(lnw["eps"][:, :], LN_EPS / (ALPHA * ALPHA)), writes=["lnw"])


def _down_phase(S, ps, wring, zt, xs, stt, lnw, lhs_fn, lhs_key, chunks, wd_v, coef, xtok, out, t0, cnt, out_ops,
                xo=None, wloader=None, defer=None):
    nk_last = chunks[-1][1] - 1
    for n in range(4):
        pbase = (n % 2) * 4
        for (k0, k1) in chunks:
            wt, wk = wring.next()
            nk = k1 - k0
            wv = wt[:, :nk * 512].rearrange("p (kt c) -> p kt c", kt=nk)
            if wloader is not None:
                wloader(wt, wk, n, k0, k1)
            else:
                S.dma("pool", lambda e, wv=wv, k0=k0, k1=k1, n=n: e.dma_start(
                    out=wv, in_=wd_v[:, k0:k1, n * 512:(n + 1) * 512]), writes=[wk])
            for t in range(4):
                for kt in range(k0, k1):
                    S.add("pe", lambda e, t=t, kt=kt, wv=wv, k0=k0, pbase=pbase: e.matmul(
                        ps[pbase + t][:, :], lhs_fn(kt, t), wv[:, kt - k0, :],
                        start=(kt == 0), stop=(kt == nk_last)),
                        reads=[wk, lhs_key(kt)], writes=[("ps", pbase + t)])
        for t in range(4):
            x_s, xk = xs[cnt["xsi"] % 2], ("xs", cnt["xsi"] % 2)
            cnt["xsi"] += 1
            S.dma("sp", lambda e, x_s=x_s, t=t, n=n: e.dma_start(
                out=x_s[:, :], in_=xtok[t0 + t * 128:t0 + (t + 1) * 128, n * 512:(n + 1) * 512]),
                reads=[("dram", "xres_in")], writes=[xk])
            S.add("dve", lambda e, x_s=x_s, t=t, n=n, pbase=pbase: e.scalar_tensor_tensor(
                zt[t][:, n * 512:(n + 1) * 512], ps[pbase + t][:, :], coef, x_s[:, :],
                ALU.mult, ALU.add),
                reads=[("ps", pbase + t), xk], writes=[("z", t)])
            if n == 3:
                tg = t0 + t * 128
                xt = None
                if xo is not None:
                    xt = {"ps": ps, "ident": xo["ident"], "stage": xo["stage"], "t": t}
                    if defer is not None:
                        xt["banks"] = (6, 7)
                oq = "pool" if defer is not None else "sp"
                li = cnt["lni"] % 2
                cnt["lni"] += 1

                def tail(t=t, tg=tg, xt=xt, oq=oq, li=li):
                    _ln_tail(S, zt[t], ("z", t), stt[li], lnw, out[tg:tg + 128, :],
                             ("dram_out", tg), out_ops, li, xt, oq)
                if defer is not None:
                    defer.append(tail)
                else:
                    tail()

    def fin():
        if xo is not None:
            oq = "pool" if defer is not None else "sp"
            o = S.dma(oq, lambda e: e.dma_start(out=xo["xTb"].rearrange("(kt p) t -> p kt t", p=128), in_=xo["stage"][:, :, :]),
                      reads=["xstage"], writes=[xo["key"]])
            out_ops.append(o)
            if xo.get("cc") is not None:
                csrc, cdst, ckey = xo["cc"]
                S.cc("pool", lambda e: e.collective_compute("AllGather", ALU.bypass, replica_groups=PAIRS,
                                                             ins=[csrc], outs=[cdst]),
                     reads=[xo["key"]], writes=[ckey], inc=1)
    if defer is not None:
        defer.append(fin)
    else:
        fin()


PAIRS = [[0, 1], [2, 3], [4, 5], [6, 7]]


def emit_ffn(S, sb, ps, c):
    nb = c["ntok"] // TB
    wg_v = c["wg"].rearrange("(kt p) c -> p kt c", p=128)
    wu_v = c["wu"].rearrange("(kt p) c -> p kt c", p=128)
    wd_v = c["wd"].rearrange("(kt p) c -> p kt c", p=128)
    xTs = sb("xTs", [128, 16, TB], BF16)
    hT = sb("hT", [128, 44, TB], BF16)
    wring = Ring("w", [sb("wr%d" % i, [128, 8192], BF16) for i in range(4)])
    sig = [sb("sig%d" % i, [128, TB], F32) for i in range(2)]
    xs = [sb("xs%d" % i, [128, 512], F32) for i in range(2)]
    zt = [sb("z%d" % i, [128, D_MODEL], F32) for i in range(4)]
    lnw, stt = _alloc_ln(sb, 2)
    stage = sb("xstage", [128, 16, TB], BF16)
    ident = sb("identf", [128, 128], F32)
    S.dma("sp", lambda e: e.dma_start(out=ident[:, :], in_=c["consts"][:, C_ID:C_ID + 128]), writes=["ident"])
    _load_ln(S, lnw, c["lng"], c["lnb"])
    out_ops = c["out_ops"]
    psi = [0]
    cnt = {"xsi": 0, "lni": 0}
    use_defer = c.get("ws") is not None
    pending = []

    def load_x(bt):
        q, xap, xkey = c["xT_src"](bt)
        if q == "sp" and bt > 0:
            q = "act"
        S.dma(q, lambda e, xap=xap: e.dma_start(out=xTs[:, :, :], in_=xap), reads=[xkey], writes=["xTs"])

    bg = c.get("bg") or []

    def gate_up(bt):
        for cp in range(22):
            if bg and cp % 3 == 0:
                bg.pop(0)()
            wt, wk = wring.next()
            wv = wt[:, :].rearrange("p (m kt c) -> p m kt c", m=2, kt=16)
            ws = c.get("ws")
            if ws is None or (bt == 0 and not c.get("ws_ready")):
                S.dma("pool", lambda e, wv=wv, cp=cp: e.dma_start(out=wv[:, 0], in_=wg_v[:, :, cp * 256:(cp + 1) * 256]),
                      writes=[wk])
                S.dma("pool", lambda e, wv=wv, cp=cp: e.dma_start(out=wv[:, 1], in_=wu_v[:, :, cp * 256:(cp + 1) * 256]),
                      writes=[wk])
                if ws is not None:
                    S.dma("sp", lambda e, wt=wt, cp=cp: e.dma_start(out=ws[cp], in_=wt[:, :]), reads=[wk],
                          writes=[("dram", "WS", cp)])
            else:
                S.dma("sp", lambda e, wt=wt, cp=cp: e.dma_start(out=wt[:, :], in_=ws[cp]), reads=[("dram", "WS", cp)],
                      writes=[wk])
            for ci in range(2):
                cc_ = cp * 2 + ci
                b0, b1 = psi[0] % 6, (psi[0] + 1) % 6
                pg, pu = ps[b0], ps[b1]
                kg, ku = ("ps", b0), ("ps", b1)
                psi[0] += 2
                for m, (pp_, kk) in enumerate(((pg, kg), (pu, ku))):
                    for kt in range(16):
                        S.add("pe", lambda e, pp_=pp_, wv=wv, m=m, kt=kt, ci=ci: e.matmul(
                            pp_[:, :], wv[:, m, kt, ci * 128:(ci + 1) * 128], xTs[:, kt, :],
                            start=(kt == 0), stop=(kt == 15)),
                            reads=[wk, "xTs"], writes=[kk])
                sg = sig[cc_ % 2]
                S.add("act", lambda e, sg=sg, pg=pg: e.activation(sg[:, :], pg[:, :], AF.Silu),
                      reads=[kg], writes=[("sig", cc_ % 2)])
                S.add("dve", lambda e, sg=sg, pu=pu, cc_=cc_: e.tensor_tensor(hT[:, cc_, :], sg[:, :], pu[:, :], ALU.mult),
                      reads=[("sig", cc_ % 2), ku], writes=[("hT", cc_)])

    load_x(0)
    for bt in range(nb):
        t0 = bt * TB
        if pending:
            S.stream("T")
            for fn_ in pending:
                fn_()
            pending = []
            S.stream("G")
            gate_up(bt)
            S.merge_streams()
        else:
            gate_up(bt)
        if bt + 1 < nb:
            load_x(bt + 1)
        xo = c["xo"](bt)
        if xo is not None:
            xo = dict(xo, ident=ident[:, :], stage=stage)
        wloader = None
        if c.get("ws") is not None:
            def wloader(wt, wk, n, k0, k1, bt=bt):
                ws = c["ws"]
                si = 22 + n * 3 + k0 // 16
                nk = k1 - k0
                if bt == 0 and not c.get("ws_ready"):
                    wv = wt[:, :nk * 512].rearrange("p (kt c) -> p kt c", kt=nk)
                    S.dma("pool", lambda e: e.dma_start(out=wv, in_=wd_v[:, k0:k1, n * 512:(n + 1) * 512]), writes=[wk])
                    S.dma("sp", lambda e: e.dma_start(out=ws[si][:, :nk * 512], in_=wt[:, :nk * 512]), reads=[wk],
                          writes=[("dram", "WS", si)])
                else:
                    S.dma("sp", lambda e: e.dma_start(out=wt[:, :nk * 512], in_=ws[si][:, :nk * 512]),
                          reads=[("dram", "WS", si)], writes=[wk])
        defer = pending if use_defer else None
        _down_phase(S, ps, wring, zt, xs, stt, lnw, lambda kt, t: hT[:, kt, t * 128:(t + 1) * 128],
                    lambda kt: ("hT", kt), ((0, 16), (16, 32), (32, 44)), wd_v, 0.5 / ALPHA, c["xres_in"], c["out_res"],
                    t0, cnt, out_ops, xo, wloader, defer)
    for fn_ in pending:
        fn_()
    while bg:
        bg.pop(0)()


def emit_wout(S, sb, ps, c):
    nb = c["ntok"] // TB
    wo_v = c["wo"].rearrange("(kt p) c -> p kt c", p=128)
    yTs = [sb("yTs%d" % i, [128, 16, TB], BF16) for i in range(2)]
    yalt = sb("yalt", [128, 16, TB], BF16)
    wres = [sb("wr%d" % i, [128, 16, 512], BF16) for i in range(4)]
    xs = [sb("xs%d" % i, [128, 512], F32) for i in range(4)]
    zt = [sb("z%d" % i, [128, D_MODEL], F32) for i in range(4)]
    lnw, stt = _alloc_ln(sb, 2)
    stage = sb("xstage", [128, 16, TB], BF16)
    ident = sb("identf", [128, 128], F32)
    flg = sb("flags", [128, 2], F32)
    S.dma("sp", lambda e: e.dma_start(out=ident[:, :], in_=c["consts"][:, C_ID:C_ID + 128]), writes=["ident"])
    S.dma("sp", lambda e: e.dma_start(out=flg[:, :], in_=c["flags"]), writes=["flags"])
    _load_ln(S, lnw, c["lng"], c["lnb"])
    for n in range(4):
        if c.get("wso") is not None:
            S.dma("sp", lambda e, n=n: e.dma_start(out=wres[n][:, :, :],
                                                   in_=c["wso"][n].rearrange("p (kt c) -> p kt c", kt=16)),
                  writes=[("wres", n)])
        else:
            S.dma("pool", lambda e, n=n: e.dma_start(out=wres[n][:, :, :], in_=wo_v[:, :, n * 512:(n + 1) * 512]),
                  writes=[("wres", n)])
    out_ops = c["out_ops"]
    bg = c.get("bg") or []
    coef = 1.0 / ALPHA
    pend = []
    pcount = 0
    xsi = 0
    lni = 0

    def load_y(bt):
        yb = yTs[bt % 2]
        y0, y0k = c["yg"](bt)
        y1, y1k = c["yg"](4 + bt)
        S.dma("sp", lambda e, yb=yb, y0=y0: e.dma_start(out=yb[:, :, :], in_=y0.rearrange("(kt p) t -> p kt t", p=128)),
              reads=[y0k], writes=[("yTs", bt % 2)])
        S.dma("sp", lambda e, y1=y1: e.dma_start(out=yalt[:, :, :], in_=y1.rearrange("(kt p) t -> p kt t", p=128)),
              reads=[y1k], writes=["yalt"])
        S.add("dve", lambda e: e.tensor_scalar(yalt[:, :, :], yalt[:, :, :], flg[:, 1:2], None, ALU.mult),
              reads=["yalt", "flags"], writes=["yalt"])
        S.add("dve", lambda e, yb=yb: e.scalar_tensor_tensor(yb[:, :, :], yb[:, :, :], flg[:, 0:1], yalt[:, :, :],
                                                              ALU.mult, ALU.add),
              reads=["yalt", "flags", ("yTs", bt % 2)], writes=[("yTs", bt % 2)])

    load_y(0)
    for bt in range(nb):
        t0 = bt * TB
        yb = yTs[bt % 2]
        if bt + 1 < nb:
            load_y(bt + 1)
        for _ in range((len(bg) + (nb - bt) - 1) // (nb - bt) if bg else 0):
            bg.pop(0)()
        xo = c["xo"](bt)
        for t in range(4):
            z = zt[t]
            for n in range(4):
                bk = pcount % 6
                pcount += 1
                for kt in range(16):
                    S.add("pe", lambda e, bk=bk, kt=kt, t=t, n=n, yb=yb: e.matmul(
                        ps[bk][:, :], yb[:, kt, t * 128:(t + 1) * 128], wres[n][:, kt, :],
                        start=(kt == 0), stop=(kt == 15)),
                        reads=[("wres", n), ("yTs", bt % 2)], writes=[("ps", bk)])
                x_s, xk = xs[xsi % 4], ("xs", xsi % 4)
                xsi += 1
                S.dma("sp", lambda e, x_s=x_s, t=t, n=n, t0=t0: e.dma_start(
                    out=x_s[:, :], in_=c["xres_in"][t0 + t * 128:t0 + (t + 1) * 128, n * 512:(n + 1) * 512]),
                    reads=[("dram", "xres_in")], writes=[xk])
                S.add("act", lambda e, z=z, n=n, bk=bk: e.mul(z[:, n * 512:(n + 1) * 512], ps[bk][:, :], coef),
                      reads=[("ps", bk)], writes=[("z", t)])
                S.add("pool", lambda e, x_s=x_s, z=z, n=n: e.tensor_tensor(
                    z[:, n * 512:(n + 1) * 512], z[:, n * 512:(n + 1) * 512], x_s[:, :], ALU.add),
                    reads=[("z", t), xk], writes=[("z", t)])
            tg = t0 + t * 128
            xt = None
            if xo is not None:
                xt = {"ps": ps, "ident": ident[:, :], "stage": stage, "t": t, "banks": (6, 7)}
            li = lni % 2
            lni += 1

            _ln_tail(S, z, ("z", t), stt[li], lnw, c["out_res"][tg:tg + 128, :], ("dram_out", tg), out_ops, li, None,
                     "pool", "pool")

            def tail(t=t, tg=tg, xt=xt, li=li, z=z, last=(t == 3), xo=xo):
                if xt is not None:
                    _ln_transposes(S, z, ("z", t), xt)
                if last and xo is not None:
                    o = S.dma("pool", lambda e: e.dma_start(out=xo["xTb"].rearrange("(kt p) t -> p kt t", p=128),
                                                          in_=stage[:, :, :]), reads=["xstage"], writes=[xo["key"]])
                    out_ops.append(o)
            pend.append(tail)
            if len(pend) > 2:
                pend.pop(0)()
    while pend:
        pend.pop(0)()
    while bg:
        bg.pop(0)()


GT = 512
HC = 32
MASKV = -1e30
PP_RG = 0
PP_CV = 16
PP_CVW = 22
PP_HG = 84
NPP = 90
C_ID, C_TRI, C_BD, C_CM, C_RM = 0, 128, 256, 384, 896
NCONST = 900


def make_consts():
    c = np.zeros((128, NCONST), np.float32)
    p = np.arange(128)[:, None]
    f = np.arange(128)[None, :]
    c[:, C_ID:C_ID + 128] = (p == f)
    c[:, C_TRI:C_TRI + 128] = (p <= f)
    c[:, C_BD:C_BD + 128] = (p <= f) & ((p // HC) == (f // HC))
    t = np.arange(512)
    c[:, C_CM:C_CM + 512] = (t % HC != 0)[None, :]
    for k in range(4):
        c[:, C_RM + k] = ((np.arange(128) // HC) == k)
    return c


def emit_mixer(S, sb, ps, cfg, parts="ABCD"):
    seq = cfg["seq"]
    ng = seq // GT
    win_v = cfg["win"].rearrange("(kt p) c -> p kt c", p=128)
    pp_d, cst_d, rgw_d, hgn_d = cfg["pp"], cfg["consts"], cfg["rgw"], cfg["hgn"]
    scale = 128 ** -0.5
    out_ops = cfg["out_ops"]
    if True:
        if True:
            pass
        pp = sb("pp", [128, NPP])
        cst = sb("cst", [128, NCONST])
        S.dma("sp", lambda e: e.dma_start(out=pp[:, :], in_=pp_d), writes=["pp"])
        S.dma("sp", lambda e: e.dma_start(out=cst[:, :], in_=cst_d), writes=["cst"])
        identb = sb("identb", [128, 128], BF16)
        trib = sb("trib", [128, 128], BF16)
        bdm = sb("bdm", [128, 128])
        ones_f = sb("ones_f", [128, 128])
        S.add("dve", lambda e: e.tensor_copy(identb[:, :], cst[:, C_ID:C_ID + 128]), reads=["cst"], writes=["identb"])
        S.add("dve", lambda e: e.tensor_copy(trib[:, :], cst[:, C_TRI:C_TRI + 128]), reads=["cst"], writes=["trib"])
        S.add("dve", lambda e: e.tensor_copy(bdm[:, :], cst[:, C_BD:C_BD + 128]), reads=["cst"], writes=["bdm"])
        S.add("dve", lambda e: e.memset(ones_f[:, :], 1.0 / 128.0), writes=["ones_f"])
        cmask = cst[:, C_CM:C_CM + 512]
        rgw = sb("rgw", [128, 4, 128], BF16)
        S.dma("pool", lambda e: e.dma_start(out=rgw[:, :, :], in_=rgw_d.rearrange("p (m o) -> p m o", m=4)),
              writes=["rgw"])
        hgn = sb("hgn", [128, 256])
        S.dma("sp", lambda e: e.dma_start(out=hgn[:, :], in_=hgn_d.partition_broadcast(128)), writes=["hgn"])
        der = sb("der", [128, 16])
        tmpd = sb("tmpd", [128, 8])
        S.add("dve", lambda e: e.memset(der[:, 10:11], LN_EPS), writes=["der_c"])
        S.add("dve", lambda e: e.memset(der[:, 11:12], 1.0), writes=["der_c"])
        for j in range(2):
            lam = pp[:, PP_RG + 8 * j + 7:PP_RG + 8 * j + 8]
            S.add("act", lambda e, j=j, lam=lam: e.activation(tmpd[:, j:j + 1], lam, AF.Exp, scale=-1.0),
                  reads=["pp"], writes=[("tmpd", j)])
            S.add("act", lambda e, j=j: e.activation(tmpd[:, 2 + j:3 + j], tmpd[:, j:j + 1], AF.Ln, bias=der[:, 11:12], scale=1.0),
                  reads=[("tmpd", j), "der_c"], writes=[("tmpd2", j)])
            S.add("dve", lambda e, j=j: e.tensor_scalar(der[:, j:j + 1], tmpd[:, 2 + j:3 + j], -8.0, None, ALU.mult),
                  reads=[("tmpd2", j)], writes=["der"])
            S.add("dve", lambda e, j=j: e.tensor_scalar(der[:, 2 + j:3 + j], tmpd[:, 2 + j:3 + j], -16.0, None, ALU.mult),
                  reads=[("tmpd2", j)], writes=["der"])
            l0 = pp[:, PP_HG + 3 * j:PP_HG + 3 * j + 1]
            l1 = pp[:, PP_HG + 3 * j + 1:PP_HG + 3 * j + 2]
            fl = pp[:, PP_HG + 3 * j + 2:PP_HG + 3 * j + 3]
            S.add("dve", lambda e, j=j, l0=l0, l1=l1: e.tensor_tensor(tmpd[:, 4 + j:5 + j], l1, l0, ALU.subtract),
                  reads=["pp"], writes=[("tmpd3", j)])
            S.add("act", lambda e, j=j: e.activation(tmpd[:, 6 + j:7 + j], tmpd[:, 4 + j:5 + j], AF.Sigmoid),
                  reads=[("tmpd3", j)], writes=[("tmpd4", j)])
            S.add("dve", lambda e, j=j, fl=fl: e.tensor_tensor(der[:, 4 + j:5 + j], tmpd[:, 6 + j:7 + j], fl, ALU.mult),
                  reads=[("tmpd4", j), "pp"], writes=["der"])
            S.add("dve", lambda e, j=j: e.tensor_scalar(der[:, 6 + j:7 + j], der[:, 4 + j:5 + j], -1.0, 1.0, ALU.mult, ALU.add),
                  reads=["der"], writes=["der"])
            S.add("dve", lambda e, j=j: e.tensor_scalar(der[:, 8 + j:9 + j], der[:, 4 + j:5 + j], 1.0, -1.0, ALU.mult, ALU.add),
                  reads=["der"], writes=["der"])
        cvd = sb("cvd", [128, 2, 31, 128], BF16)
        for j in range(2):
            for k in range(31):
                S.add("dve", lambda e, j=j, k=k: e.tensor_scalar(
                    cvd[:, j, k, :], cst[:, C_ID:C_ID + 128], pp[:, PP_CVW + 31 * j + k:PP_CVW + 31 * j + k + 1], None, ALU.mult),
                    reads=["cst", "pp"], writes=["cvd"])
        xTs2 = [sb("xTs%d" % i_, [128, 16, GT], BF16) for i_ in range(2)]
        wrings = {"X": Ring("wX", [sb("wrX%d" % i, [128, 16, 256], BF16) for i in range(3)]),
                  "Y": Ring("wY", [sb("wrY%d" % i, [128, 16, 256], BF16) for i in range(3)])}
        cur = {"s": "X"}
        kT = sb("kT", [128, 2, seq], BF16)
        vaug = sb("vaug", [128, 2, seq // 128, 130], BF16)
        kmT = sb("kmT", [128, 2, 16], BF16)
        qT = sb("qT", [128, 2, GT], BF16)
        S.add("dve", lambda e: e.memset(vaug[:, :, :, 128:130], 1.0), writes=["vaug_ones"])
        kmf = sb("kmf", [128, 2, 16])
        S.add("dve", lambda e: e.memset(kmf[:, :, :], 0.0), writes=[("kmf", 0), ("kmf", 1)])
        pring = Ring("pT", [sb("pT%d" % i, [128, GT], BF16) for i in range(4)])
        acc = sb("acc", [128, 4, 130])
        gsb = sb("gsb", [128, 16])
        top8 = sb("top8", [128, 8])
        sel = sb("sel", [128, 4, 16])
        ksum = sb("ksum", [128, 2])
        rcp = sb("rcp", [128, 4])
        ystgs = {s_: Ring("ystg" + s_, [sb("ystg%s%d" % (s_, i), [128, 512]) for i in range(1)]) for s_ in "XY"}
        frings = {"X": Ring("fX", [sb("fX%d" % i, [128, GT]) for i in range(8)]),
                  "Y": Ring("fY", [sb("fY%d" % i, [128, GT]) for i in range(7)])}
        brings = {"X": Ring("bX", [sb("bbX%d" % i, [128, GT], BF16) for i in range(2)]),
                  "Y": Ring("bY", [sb("bbY%d" % i, [128, GT], BF16) for i in range(3)])}
        rgx = sb("rgx", [128, 2, 3 + GT])
        rgh = sb("rgh", [128, 2, 8])
        S.add("dve", lambda e: e.memset(rgx[:, :, 0:3], 0.0), writes=[("rgx", 0), ("rgx", 1)])
        S.add("dve", lambda e: e.memset(rgh[:, :, 0:1], 0.0), writes=[("rgh", 0), ("rgh", 1)])
        cvx = sb("cvx", [128, 2, 30 + GT], BF16)
        S.add("dve", lambda e: e.memset(cvx[:, :, 0:30], 0.0), writes=[("cvx", 0), ("cvx", 1)])
        S32 = sb("S32", [128, 2, 128])
        Sb = sb("Sb", [128, 2, 128], BF16)
        S.add("dve", lambda e: e.memset(S32[:, :, :], 0.0), writes=[("S32", 0), ("S32", 1)])
        S.add("dve", lambda e: e.memset(Sb[:, :, :], 0.0), writes=[("Sb", 0), ("Sb", 1)])
        qdz = sb("qdz", [128, 4, 128], BF16)
        S.add("dve", lambda e: e.memset(qdz[:, :, :], 0.0), writes=["qdz"])
        khz = sb("khz", [128, 4, 128], BF16)
        vtok = sb("vtok", [128, 4, 128], BF16)
        gtok = sb("gtok", [128, 4, 128])
        attm = sb("attm", [128, 128], BF16)
        hsm = sb("hsm", [128, 8])
        prings = {("pj", "X"): Ring("ps", [(ps[0], 0)]), ("pm", "X"): Ring("ps", [(ps[1], 1), (ps[2], 2)]),
                  ("pa", "X"): Ring("ps", [(ps[3], 3), (ps[4], 4)]),
                  ("pj", "Y"): Ring("ps", [(ps[5], 5)]), ("pm", "Y"): Ring("ps", [(ps[6], 6)]),
                  ("pa", "Y"): Ring("ps", [(ps[7], 7)])}
        pj, pm, pa = "pj", "pm", "pa"

        def psn(ring):
            (t, i), _ = prings[(ring, cur["s"])].next()
            return t, ("ps", i)

        wstates = {"X": {}, "Y": {}}

        def wtile(ct):
            cp = ct // 2
            wstate = wstates[cur["s"]]
            if cp not in wstate:
                wt, wk = wrings[cur["s"]].next()
                for k_ in [k_ for k_, v_ in wstate.items() if v_[1] == wk]:
                    del wstate[k_]
                S.dma("pool", lambda e, wt=wt, cp=cp: e.dma_start(out=wt[:, :, :], in_=win_v[:, :, cp * 256:(cp + 1) * 256]),
                      writes=[wk])
                wstate[cp] = (wt, wk)
            wt, wk = wstate[cp]
            return wt[:, :, (ct % 2) * 128:(ct % 2 + 1) * 128], wk

        def proj_fm(ct):
            w, wk = wtile(ct)
            p, pk = psn(pj)
            xTs, xkey_ = gstate["xTs"]
            for kt in range(16):
                S.add("pe", lambda e, p=p, w=w, kt=kt, xTs=xTs: e.matmul(p[:, :], w[:, kt, :], xTs[:, kt, :],
                                                                          start=(kt == 0), stop=(kt == 15)),
                      reads=[wk, xkey_], writes=[pk])
            return p, pk

        def proj_tm(ct):
            w, wk = wtile(ct)
            p, pk = psn(pj)
            xTs, xkey_ = gstate["xTs"]
            for tt in range(4):
                for kt in range(16):
                    S.add("pe", lambda e, p=p, w=w, kt=kt, tt=tt, xTs=xTs: e.matmul(
                        p[:, tt * 128:(tt + 1) * 128], xTs[:, kt, tt * 128:(tt + 1) * 128], w[:, kt, :],
                        start=(kt == 0), stop=(kt == 15)),
                        reads=[wk, xkey_], writes=[pk])
            return p, pk

        def fnext():
            return frings[cur["s"]].next()

        class _RingSel:
            def __init__(self, d):
                self.d = d

            def next(self):
                return self.d[cur["s"]].next()

        bring = _RingSel(brings)
        ystg = _RingSel(ystgs)
        ybst = _RingSel({s_: Ring("ybst" + s_, [sb("ybst%s%d" % (s_, i), [128, GT], BF16) for i in range(2)]) for s_ in "XY"})
        gstate = {}

        def emit_yfm(yb_, ybk_, ci):
            ysrc, ysk_d = gstate["ysrc"]
            o = S.dma("sp", lambda e, yb_=yb_, ysrc=ysrc, ci=ci: e.dma_start(out=ysrc[ci * 128:(ci + 1) * 128, :], in_=yb_[:, :]),
                      reads=[ybk_], writes=[ysk_d])
            out_ops.append(o)
            gstate["n"] += 1

        def emit_ytok(ys, ysk, ci):
            pt_, ptk_ = psn(pm)
            for i in range(4):
                S.add("pe", lambda e, pt_=pt_, ys=ys, i=i: e.transpose(pt_[:, i * 128:(i + 1) * 128], ys[:, i * 128:(i + 1) * 128],
                                                                        cst[:, C_ID:C_ID + 128]),
                      reads=[ysk, "cst"], writes=[ptk_])
            yb_, ybk_ = ybst.next()
            S.add("act", lambda e, yb_=yb_, pt_=pt_: e.copy(yb_[:, :], pt_[:, :]), reads=[ptk_], writes=[ybk_])
            emit_yfm(yb_, ybk_, ci)

        for gi in range(ng):
            t0 = gi * GT
            def load_x(g_):
                xap, xkey = cfg["xg"](g_)
                xb_ = xTs2[g_ % 2]
                S.dma("sp", lambda e, xap=xap, xb_=xb_: e.dma_start(out=xb_[:, :, :], in_=xap), reads=[xkey],
                      writes=[("xTs", g_ % 2)])
            if gi == 0:
                load_x(0)
            if gi + 1 < ng:
                load_x(gi + 1)
            gstate["xTs"] = (xTs2[gi % 2], ("xTs", gi % 2))
            gstate["ysrc"] = cfg["ysrc"](gi)
            gstate["n"] = 0
            def unit_A(hh, gi=gi, t0=t0):
                p, pk = proj_fm(0 + hh)
                S.add("act", lambda e, p=p, hh=hh: e.copy(qT[:, hh, :], p[:, :]), reads=[pk], writes=[("qT", hh)])
                p, pk = proj_fm(2 + hh)
                S.add("act", lambda e, p=p, hh=hh, t0=t0: e.copy(kT[:, hh, t0:t0 + GT], p[:, :]), reads=[pk],
                      writes=[("kT", hh, gi)])
                for b_ in range(2):
                    jk, jkk = fnext()
                    S.add("act", lambda e, p=p, jk=jk, b_=b_, hh=hh, gi=gi: e.activation(
                        jk[:, 0:256], p[:, b_ * 256:(b_ + 1) * 256], AF.Identity, scale=1.0 / 256.0,
                        accum_out=kmf[:, hh, 2 * gi + b_:2 * gi + b_ + 1]),
                        reads=[pk], writes=[jkk, ("kmf", hh)])
                S.add("dve", lambda e, hh=hh: e.tensor_copy(kmT[:, hh, :], kmf[:, hh, :]), reads=[("kmf", hh)],
                      writes=[("kmT", hh)])
                p, pk = proj_tm(4 + hh)
                if not os.environ.get("SKIP_V"):
                    S.add("act", lambda e, p=p, hh=hh, gi=gi: e.copy(
                        vaug[:, hh, 4 * gi:4 * gi + 4, 0:128], p[:, :].rearrange("p (t c) -> p t c", t=4)),
                        reads=[pk], writes=[("vaug", hh, gi)])
                MST = int(os.environ.get('MOBA_STAGE', '9'))
                if MST < 2:
                    return
                for i in range(4):
                    blk = 2 * gi + i // 2
                    S.add("dve", lambda e: e.memset(gsb[:, :], MASKV), writes=["gsb"])
                    if blk > 0:
                        pg, pgk = psn(pm)
                        S.add("pe", lambda e, pg=pg, hh=hh, i=i: e.matmul(pg[:, 0:16], qT[:, hh, i * 128:(i + 1) * 128],
                                                                        kmT[:, hh, :], start=True, stop=True),
                              reads=[("qT", hh), ("kmT", hh)], writes=[pgk])
                        S.add("dve", lambda e, pg=pg, blk=blk: e.tensor_copy(gsb[:, 0:blk], pg[:, 0:blk]),
                              reads=[pgk], writes=["gsb"])
                    S.add("dve", lambda e: e.max(top8[:, :], gsb[:, :]), reads=["gsb"], writes=["top8"])
                    S.add("dve", lambda e, i=i: e.tensor_scalar(sel[:, i, :], gsb[:, :], top8[:, 2:3], None, ALU.is_ge),
                          reads=["gsb", "top8"], writes=[("sel", i)])
                    S.add("dve", lambda e, i=i, blk=blk: e.memset(sel[:, i, blk:blk + 1], 1.0), writes=[("sel", i)])
                if MST < 3:
                    return
                S.add("dve", lambda e: e.memset(acc[:, :, :], 0.0), writes=["acc"])
                nblk = 2 * gi + 2
                for n in range(nblk):
                    pts = []
                    for jj in range(2):
                        kt = 2 * n + jj
                        psc, psk = psn(pm)
                        S.add("pe", lambda e, psc=psc, hh=hh, kt=kt: e.matmul(
                            psc[:, :], kT[:, hh, kt * 128:(kt + 1) * 128], qT[:, hh, :], start=True, stop=True),
                            reads=[("kT", hh, kt // 4), ("qT", hh)], writes=[psk])
                        pt, ptk = pring.next()
                        S.add("act", lambda e, pt=pt, psc=psc: e.activation(pt[:, :], psc[:, :], AF.Exp, scale=scale),
                              reads=[psk], writes=[ptk])
                        j = kt - 4 * gi
                        if j >= 0:
                            S.add("dve", lambda e, pt=pt, j=j: e.tensor_tensor(
                                pt[:, j * 128:(j + 1) * 128], pt[:, j * 128:(j + 1) * 128], trib[:, :], ALU.mult),
                                reads=[ptk, "trib"], writes=[ptk])
                        pts.append((pt, ptk, kt))
                    for pair in (range(2) if MST >= 4 else ()):
                        po, pok = psn(pa)
                        used = []
                        for ii in range(2):
                            i = pair * 2 + ii
                            kts = [(pt, ptk, kt) for (pt, ptk, kt) in pts if kt <= 4 * gi + i]
                            for idx, (pt, ptk, kt) in enumerate(kts):
                                S.add("pe", lambda e, po=po, ii=ii, pt=pt, i=i, hh=hh, kt=kt, idx=idx, nk=len(kts): e.matmul(
                                    po[:, ii * 129:(ii + 1) * 129], pt[:, i * 128:(i + 1) * 128], vaug[:, hh, kt, 0:129],
                                    start=(idx == 0), stop=(idx == nk - 1)),
                                    reads=[ptk, ("vaug", hh, kt // 4), "vaug_ones"], writes=[pok])
                            if kts:
                                used.append((ii, i))
                        for ii, i in used:
                            S.add("dve", lambda e, po=po, ii=ii, i=i, n=n: e.scalar_tensor_tensor(
                                acc[:, i, 0:129], po[:, ii * 129:(ii + 1) * 129], sel[:, i, n:n + 1], acc[:, i, 0:129],
                                ALU.mult, ALU.add),
                                reads=[pok, ("sel", i), "acc"], writes=["acc"])
                if MST < 5:
                    return
                S.add("dve", lambda e: e.reciprocal(rcp[:, :], acc[:, :, 128]), reads=["acc"], writes=["rcp"])
                ys, ysk = ystg.next()
                for i in range(4):
                    S.add("dve", lambda e, i=i, ys=ys: e.tensor_scalar(ys[:, i * 128:(i + 1) * 128], acc[:, i, 0:128],
                                                                      rcp[:, i:i + 1], None, ALU.mult),
                          reads=["acc", "rcp"], writes=[ysk])
                emit_ytok(ys, ysk, 0 + hh)
            def unit_B(j, gi=gi, t0=t0):
                pc = PP_RG + 8 * j
                pgate, pgk = proj_fm(6 + j)
                g0, g0k = fnext()
                S.add("act", lambda e, g0=g0, pgate=pgate: e.copy(g0[:, :], pgate[:, :]), reads=[pgk], writes=[g0k])
                g1, g1k = fnext()
                S.add("act", lambda e, g1=g1, g0=g0: e.activation(g1[:, :], g0[:, :], AF.Square), reads=[g0k], writes=[g1k])
                S.add("dve", lambda e, g1=g1: e.tensor_scalar(g1[:, :], g1[:, :], 0.044715, 1.0, ALU.mult, ALU.add),
                      reads=[g1k], writes=[g1k])
                S.add("dve", lambda e, g1=g1, g0=g0: e.tensor_tensor(g1[:, :], g1[:, :], g0[:, :], ALU.mult),
                      reads=[g1k, g0k], writes=[g1k])
                S.add("act", lambda e, g1=g1: e.activation(g1[:, :], g1[:, :], AF.Sigmoid, scale=1.5957691216057308),
                      reads=[g1k], writes=[g1k])
                S.add("dve", lambda e, g1=g1, g0=g0: e.tensor_tensor(g1[:, :], g1[:, :], g0[:, :], ALU.mult),
                      reads=[g1k, g0k], writes=[g1k])
                px, pxk = proj_fm(8 + j)
                S.add("act", lambda e, px=px, j=j: e.copy(rgx[:, j, 3:3 + GT], px[:, :]), reads=[pxk], writes=[("rgx", j)])
                xc, xck = fnext()
                S.add("dve", lambda e, xc=xc, j=j, pc=pc: e.tensor_scalar(
                    xc[:, :], rgx[:, j, 0:GT], pp[:, pc:pc + 1], pp[:, pc + 4:pc + 5], ALU.mult, ALU.add),
                    reads=[("rgx", j), "pp"], writes=[xck])
                for k in range(1, 4):
                    S.add("dve", lambda e, xc=xc, j=j, pc=pc, k=k: e.scalar_tensor_tensor(
                        xc[:, :], rgx[:, j, k:k + GT], pp[:, pc + k:pc + k + 1], xc[:, :], ALU.mult, ALU.add),
                        reads=[("rgx", j), "pp", xck], writes=[xck])
                S.add("act", lambda e, j=j: e.copy(rgx[:, j, 0:3], rgx[:, j, GT:GT + 3]), reads=[("rgx", j), xck],
                      writes=[("rgx", j)])
                xcb, xcbk = bring.next()
                S.add("act", lambda e, xcb=xcb, xc=xc: e.copy(xcb[:, :], xc[:, :]), reads=[xck], writes=[xcbk])
                pr, prk = psn(pm)
                S.add("pe", lambda e, pr=pr, j=j, xcb=xcb: e.matmul(pr[:, :], rgw[:, j, :], xcb[:, :], start=True, stop=True),
                      reads=["rgw", xcbk], writes=[prk])
                pi, pik = psn(pm)
                S.add("pe", lambda e, pi=pi, j=j, xcb=xcb: e.matmul(pi[:, :], rgw[:, 2 + j, :], xcb[:, :], start=True, stop=True),
                      reads=["rgw", xcbk], writes=[pik])
                r, rk = fnext()
                S.add("act", lambda e, r=r, pr=pr, pc=pc: e.activation(r[:, :], pr[:, :], AF.Sigmoid, bias=pp[:, pc + 5:pc + 6], scale=1.0),
                      reads=[prk, "pp"], writes=[rk])
                ig, igk = fnext()
                S.add("act", lambda e, ig=ig, pi=pi, pc=pc: e.activation(ig[:, :], pi[:, :], AF.Sigmoid, bias=pp[:, pc + 6:pc + 7], scale=1.0),
                      reads=[pik, "pp"], writes=[igk])
                a, ak = fnext()
                S.add("act", lambda e, a=a, r=r, j=j: e.activation(a[:, :], r[:, :], AF.Exp, scale=der[:, j:j + 1]),
                      reads=[rk, "der"], writes=[ak])
                S.add("act", lambda e, r=r, j=j: e.activation(r[:, :], r[:, :], AF.Exp, scale=der[:, 2 + j:3 + j]),
                      reads=[rk, "der"], writes=[rk])
                S.add("dve", lambda e, r=r: e.tensor_scalar(r[:, :], r[:, :], -1.0, 1.0, ALU.mult, ALU.add),
                      reads=[rk], writes=[rk])
                S.add("dve", lambda e, r=r: e.tensor_scalar(r[:, :], r[:, :], 1e-30, None, ALU.max), reads=[rk], writes=[rk])
                S.add("act", lambda e, r=r: e.activation(r[:, :], r[:, :], AF.Sqrt), reads=[rk], writes=[rk])
                S.add("dve", lambda e, ig=ig, xc=xc: e.tensor_tensor(ig[:, :], ig[:, :], xc[:, :], ALU.mult),
                      reads=[igk, xck], writes=[igk])
                S.add("dve", lambda e, ig=ig, r=r: e.tensor_tensor(ig[:, :], ig[:, :], r[:, :], ALU.mult),
                      reads=[igk, rk], writes=[igk])
                h, hk = fnext()
                S.add("dve", lambda e, h=h, a=a, ig=ig, j=j: e.tensor_tensor_scan(
                    h[:, :], a[:, :], ig[:, :], rgh[:, j, 0:1], ALU.mult, ALU.add),
                    reads=[ak, igk, ("rgh", j)], writes=[hk])
                S.add("act", lambda e, h=h, j=j: e.copy(rgh[:, j, 0:1], h[:, GT - 1:GT]), reads=[hk], writes=[("rgh", j)])
                yb_, ybk_ = ybst.next()
                S.add("dve", lambda e, h=h, g1=g1, yb_=yb_: e.tensor_tensor(yb_[:, :], h[:, :], g1[:, :], ALU.mult),
                      reads=[hk, g1k], writes=[ybk_])
                emit_yfm(yb_, ybk_, 4 + j)
            def unit_C(j, gi=gi, t0=t0):
                pc = PP_CV + 3 * j
                pgt, pgtk = proj_fm(12 + j)
                sg, sgk = fnext()
                S.add("act", lambda e, sg=sg, pgt=pgt: e.activation(sg[:, :], pgt[:, :], AF.Sigmoid), reads=[pgtk], writes=[sgk])
                pval, pvk = proj_fm(10 + j)
                S.add("dve", lambda e, sg=sg, pval=pval, j=j: e.tensor_tensor(cvx[:, j, 30:30 + GT], sg[:, :], pval[:, :], ALU.mult),
                      reads=[sgk, pvk], writes=[("cvx", j)])
                pcv, pcvk = psn(pm)
                for k in range(31):
                    S.add("pe", lambda e, pcv=pcv, j=j, k=k: e.matmul(pcv[:, :], cvd[:, j, k, :], cvx[:, j, k:k + GT],
                                                                      start=(k == 0), stop=(k == 30)),
                          reads=["cvd", ("cvx", j)], writes=[pcvk])
                S.add("act", lambda e, j=j: e.copy(cvx[:, j, 0:30], cvx[:, j, GT:GT + 30]), reads=[("cvx", j), pcvk],
                      writes=[("cvx", j)])
                u, uk = fnext()
                S.add("act", lambda e, u=u, pcv=pcv, pc=pc: e.activation(u[:, :], pcv[:, :], AF.Identity, bias=pp[:, pc:pc + 1], scale=1.0),
                      reads=[pcvk, "pp"], writes=[uk])
                u2, u2k = fnext()
                S.add("act", lambda e, u=u, u2=u2: e.activation(u2[:, :], u[:, :], AF.Square), reads=[uk], writes=[u2k])
                pmn, pmnk = psn(pm)
                S.add("pe", lambda e, pmn=pmn, u=u: e.matmul(pmn[:, :], ones_f[:, :], u[:, :], start=True, stop=True),
                      reads=["ones_f", uk], writes=[pmnk])
                mn, mnk = fnext()
                S.add("act", lambda e, mn=mn, pmn=pmn: e.copy(mn[:, :], pmn[:, :]), reads=[pmnk], writes=[mnk])
                pm2, pm2k = psn(pm)
                S.add("pe", lambda e, pm2=pm2, u2=u2: e.matmul(pm2[:, :], ones_f[:, :], u2[:, :], start=True, stop=True),
                      reads=["ones_f", u2k], writes=[pm2k])
                S.add("dve", lambda e, u2=u2, mn=mn: e.tensor_tensor(u2[:, :], mn[:, :], mn[:, :], ALU.mult),
                      reads=[mnk, u2k], writes=[u2k])
                S.add("dve", lambda e, u2=u2, pm2=pm2: e.tensor_tensor(u2[:, :], pm2[:, :], u2[:, :], ALU.subtract),
                      reads=[pm2k, u2k], writes=[u2k])
                S.add("dve", lambda e, u2=u2: e.tensor_scalar(u2[:, :], u2[:, :], 0.0, None, ALU.max), reads=[u2k], writes=[u2k])
                S.add("act", lambda e, u2=u2: e.activation(u2[:, :], u2[:, :], AF.Ln, bias=der[:, 10:11], scale=1.0),
                      reads=[u2k, "der_c"], writes=[u2k])
                S.add("act", lambda e, u2=u2: e.activation(u2[:, :], u2[:, :], AF.Exp, scale=-0.5), reads=[u2k], writes=[u2k])
                S.add("dve", lambda e, u=u, mn=mn: e.tensor_tensor(u[:, :], u[:, :], mn[:, :], ALU.subtract),
                      reads=[uk, mnk], writes=[uk])
                S.add("dve", lambda e, u=u, u2=u2: e.tensor_tensor(u[:, :], u[:, :], u2[:, :], ALU.mult),
                      reads=[uk, u2k], writes=[uk])
                yb_, ybk_ = ybst.next()
                S.add("act", lambda e, u=u, pc=pc, yb_=yb_: e.activation(yb_[:, :], u[:, :], AF.Silu, bias=pp[:, pc + 2:pc + 3],
                                                                         scale=pp[:, pc + 1:pc + 2]),
                      reads=[uk, "pp"], writes=[ybk_])
                emit_yfm(yb_, ybk_, 6 + j)
            def unit_D(hh, gi=gi, t0=t0):
                lb, oml, noml = der[:, 4 + hh:5 + hh], der[:, 6 + hh:7 + hh], der[:, 8 + hh:9 + hh]
                pq, pqk = proj_fm(14 + hh)
                qf, qfk = fnext()
                S.add("act", lambda e, qf=qf, pq=pq: e.copy(qf[:, :], pq[:, :]), reads=[pqk], writes=[qfk])
                pf, pfk = proj_fm(16 + hh)
                sg, sgk = fnext()
                S.add("act", lambda e, sg=sg, pf=pf: e.activation(sg[:, :], pf[:, :], AF.Sigmoid), reads=[pfk], writes=[sgk])
                lf, lfk = fnext()
                S.add("dve", lambda e, lf=lf, sg=sg, oml=oml, lb=lb: e.tensor_scalar(lf[:, :], sg[:, :], oml, lb, ALU.mult, ALU.add),
                      reads=[sgk, "der"], writes=[lfk])
                S.add("act", lambda e, lf=lf: e.activation(lf[:, :], lf[:, :], AF.Ln), reads=[lfk], writes=[lfk])
                S.add("dve", lambda e, sg=sg, oml=oml, noml=noml: e.tensor_scalar(sg[:, :], sg[:, :], noml, oml, ALU.mult, ALU.add),
                      reads=[sgk, "der"], writes=[sgk])
                bcs, bck = fnext()
                S.add("dve", lambda e, bcs=bcs, lf=lf: e.tensor_tensor_scan(bcs[:, :], cmask, lf[:, :], 0.0, ALU.mult, ALU.add),
                      reads=[lfk, "cst"], writes=[bck])
                eb, ebk = fnext()
                S.add("act", lambda e, eb=eb, bcs=bcs: e.activation(eb[:, :], bcs[:, :], AF.Exp), reads=[bck], writes=[ebk])
                S.add("dve", lambda e, lf=lf, bcs=bcs: e.tensor_tensor(
                    lf[:, :].rearrange("p (c k) -> p c k", k=HC),
                    bcs[:, :].rearrange("p (c k) -> p c k", k=HC)[:, :, HC - 1:HC].broadcast_to([128, GT // HC, HC]),
                    bcs[:, :].rearrange("p (c k) -> p c k", k=HC), ALU.subtract),
                    reads=[bck, lfk], writes=[lfk])
                S.add("act", lambda e, lf=lf: e.activation(lf[:, :], lf[:, :], AF.Exp), reads=[lfk], writes=[lfk])
                S.add("act", lambda e, bcs=bcs: e.activation(bcs[:, :], bcs[:, :], AF.Exp, scale=-1.0), reads=[bck], writes=[bck])
                qd, qdk = bring.next()
                S.add("dve", lambda e, qd=qd, qf=qf, eb=eb: e.tensor_tensor(qd[:, :], qf[:, :], eb[:, :], ALU.mult),
                      reads=[qfk, ebk], writes=[qdk])
                kd, kdk = bring.next()
                S.add("dve", lambda e, kd=kd, sg=sg, bcs=bcs: e.tensor_tensor(kd[:, :], sg[:, :], bcs[:, :], ALU.mult),
                      reads=[sgk, bck], writes=[kdk])
                kh, khk = bring.next()
                S.add("dve", lambda e, kh=kh, sg=sg, lf=lf: e.tensor_tensor(kh[:, :], sg[:, :], lf[:, :], ALU.mult),
                      reads=[sgk, lfk], writes=[khk])
                pv, pvk = proj_tm(18 + hh)
                S.add("act", lambda e, pv=pv: e.copy(vtok[:, :, :], pv[:, :].rearrange("p (t c) -> p t c", t=4)),
                      reads=[pvk], writes=["vtok"])
                pgg, pggk = proj_tm(20 + hh)
                S.add("act", lambda e, pgg=pgg: e.activation(gtok[:, :, :], pgg[:, :].rearrange("p (t c) -> p t c", t=4), AF.Silu),
                      reads=[pggk], writes=["gtok"])
                ys, ysk = ystg.next()
                for tt in range(4):
                    c0 = tt * 128
                    ptr, ptrk = psn(pm)
                    ptr_b = ptr[:, :].bitcast(BF16)
                    S.add("pe", lambda e, ptr_b=ptr_b, kh=kh, c0=c0: e.transpose(ptr_b[:, 0:128], kh[:, c0:c0 + 128], identb[:, :]),
                          reads=[khk, "identb"], writes=[ptrk])
                    for c in range(4):
                        S.add("dve", lambda e, ptr_b=ptr_b, c=c: e.tensor_scalar(
                            khz[:, c, :], ptr_b[:, 0:128], cst[:, C_RM + c:C_RM + c + 1], None, ALU.mult),
                            reads=[ptrk, "cst"], writes=[("khz", c)])
                    for c in range(4):
                        S.add("act", lambda e, qd=qd, c=c, c0=c0: e.copy(qdz[:, c, c * HC:(c + 1) * HC],
                                                                       qd[:, c0 + c * HC:c0 + (c + 1) * HC]),
                              reads=[qdk], writes=[("qdz", c)])
                    pat, patk = psn(pm)
                    S.add("pe", lambda e, pat=pat, kd=kd, qd=qd, c0=c0: e.matmul(
                        pat[:, 0:128], kd[:, c0:c0 + 128], qd[:, c0:c0 + 128], start=True, stop=True),
                        reads=[kdk, qdk], writes=[patk])
                    S.add("dve", lambda e, pat=pat: e.tensor_tensor(attm[:, :], pat[:, 0:128], bdm[:, :], ALU.mult),
                          reads=[patk, "bdm"], writes=["attm"])
                    po, pok = psn(pa)
                    S.add("pe", lambda e, po=po, tt=tt: e.matmul(po[:, 0:128], attm[:, :], vtok[:, tt, :], start=True, stop=False),
                          reads=["attm", "vtok"], writes=[pok])
                    for c in range(4):
                        S.add("pe", lambda e, po=po, c=c, hh=hh: e.matmul(po[:, 0:128], qdz[:, c, :], Sb[:, hh, :],
                                                                        start=False, stop=(c == 3)),
                              reads=[("qdz", c), ("Sb", hh)], writes=[pok])
                        pst, pstk = psn(pm)
                        S.add("pe", lambda e, pst=pst, c=c, tt=tt: e.matmul(pst[:, 0:128], khz[:, c, :], vtok[:, tt, :],
                                                                          start=True, stop=True),
                              reads=[("khz", c), "vtok"], writes=[pstk])
                        col = c0 + (c + 1) * HC - 1
                        S.add("dve", lambda e, pst=pst, hh=hh, eb=eb, col=col: e.scalar_tensor_tensor(
                            Sb[:, hh, :], S32[:, hh, :], eb[:, col:col + 1], pst[:, 0:128], ALU.mult, ALU.add),
                            reads=[pstk, ebk, ("S32", hh)], writes=[("Sb", hh)])
                        S.add("dve", lambda e, pst=pst, hh=hh, eb=eb, col=col: e.scalar_tensor_tensor(
                            S32[:, hh, :], S32[:, hh, :], eb[:, col:col + 1], pst[:, 0:128], ALU.mult, ALU.add),
                            reads=[pstk, ebk, ("S32", hh)], writes=[("S32", hh)])
                    sq, sqk = fnext()
                    S.add("act", lambda e, sq=sq, po=po, tt=tt: e.activation(sq[:, 0:128], po[:, 0:128], AF.Square,
                                                                           accum_out=hsm[:, tt:tt + 1]),
                          reads=[pok], writes=[sqk, ("hsm", tt)])
                    S.add("dve", lambda e, tt=tt: e.tensor_scalar(hsm[:, tt:tt + 1], hsm[:, tt:tt + 1], 1.0 / 128.0, LN_EPS, ALU.mult, ALU.add),
                          reads=[("hsm", tt)], writes=[("hsm", tt)])
                    S.add("act", lambda e, tt=tt: e.activation(hsm[:, tt:tt + 1], hsm[:, tt:tt + 1], AF.Sqrt),
                          reads=[("hsm", tt)], writes=[("hsm", tt)])
                    S.add("dve", lambda e, tt=tt: e.reciprocal(hsm[:, 4 + tt:5 + tt], hsm[:, tt:tt + 1]),
                          reads=[("hsm", tt)], writes=[("hsmr", tt)])
                    S.add("dve", lambda e, sq=sq, po=po, tt=tt, hh=hh: e.scalar_tensor_tensor(
                        sq[:, 0:128], po[:, 0:128], hsm[:, 4 + tt:5 + tt], hgn[:, hh * 128:(hh + 1) * 128], ALU.mult, ALU.mult),
                        reads=[pok, ("hsmr", tt), "hgn", sqk], writes=[sqk])
                    S.add("dve", lambda e, sq=sq, ys=ys, tt=tt: e.tensor_tensor(ys[:, tt * 128:(tt + 1) * 128], sq[:, 0:128],
                                                                              gtok[:, tt, :], ALU.mult),
                          reads=[sqk, "gtok"], writes=[ysk])
                emit_ytok(ys, ysk, 2 + hh)
            bg = cfg.get("bg") or []
            nbg = cfg.get("bg_per_unit", 0)

            def emit_bg():
                for _ in range(nbg):
                    if bg:
                        bg.pop(0)()
            S.stream("X")
            cur["s"] = "X"
            for u_ in range(2):
                if "A" in parts:
                    emit_bg()
                    unit_A(u_)
            for u_ in range(2):
                if "B" in parts:
                    emit_bg()
                    unit_B(u_)
            S.stream("Y")
            cur["s"] = "Y"
            for u_ in range(2):
                if "D" in parts:
                    emit_bg()
                    unit_D(u_)
            for u_ in range(2):
                if "C" in parts:
                    emit_bg()
                    unit_C(u_)
            S.merge_streams()
            if gi == ng - 1:
                while bg:
                    bg.pop(0)()
            cur["s"] = "X"
            def emit_cc(g_):
                ccg = cfg["cc"](g_)
                if ccg is not None:
                    csrc, cdst, ckey = ccg
                    S.cc("pool", lambda e, csrc=csrc, cdst=cdst: e.collective_compute(
                        "AllGather", ALU.bypass, replica_groups=PAIRS, ins=[csrc], outs=[cdst]),
                        reads=[("dram", "YS", g_)], writes=[ckey], inc=1)
            if gi > 0:
                emit_cc(gi - 1)
            if gi == ng - 1:
                emit_cc(gi)


def mixer_inputs(x_seq, g, l, w_in, rg_conv_w, rg_conv_b, rg_w_a, rg_b_a, rg_w_x, rg_b_x, rg_lambda,
                 cv_w, cv_b, cv_ln_g, cv_ln_b, hg_lower_bounds, hg_norm_g, consts):
    cols = []
    for s in range(11):
        for u in range(2):
            c0 = s * 512 + (g * 2 + u) * 128
            cols.append(np.arange(c0, c0 + 128))
    cols = np.concatenate(cols)
    win = np.ascontiguousarray(w_in[l][:, cols])
    pp = np.zeros((128, NPP), np.float32)
    for j in range(2):
        ch = g * 256 + j * 128 + np.arange(128)
        for k in range(4):
            pp[:, PP_RG + 8 * j + k] = rg_conv_w[l, k, ch]
        pp[:, PP_RG + 8 * j + 4] = rg_conv_b[l, ch]
        pp[:, PP_RG + 8 * j + 5] = rg_b_a[l, ch]
        pp[:, PP_RG + 8 * j + 6] = rg_b_x[l, ch]
        pp[:, PP_RG + 8 * j + 7] = rg_lambda[l, ch]
        pp[:, PP_CV + 3 * j] = cv_b[l, ch]
        pp[:, PP_CV + 3 * j + 1] = cv_ln_g[l, ch]
        pp[:, PP_CV + 3 * j + 2] = cv_ln_b[l, ch]
        for k in range(31):
            pp[:, PP_CVW + 31 * j + k] = cv_w[l, k, ch]
        pp[:, PP_HG + 3 * j] = hg_lower_bounds[0, ch]
        pp[:, PP_HG + 3 * j + 1] = hg_lower_bounds[1, ch]
        pp[:, PP_HG + 3 * j + 2] = float(l == 1)
    rgw = np.zeros((128, 4, 128), np.float32)
    for j in range(2):
        rgw[:, j, :] = rg_w_a[l, g * 2 + j]
        rgw[:, 2 + j, :] = rg_w_x[l, g * 2 + j]
    hgn = np.ascontiguousarray(hg_norm_g[l, g * 256:(g + 1) * 256]).reshape(1, 256)
    return {"xT": None if x_seq is None else np.ascontiguousarray(x_seq.T), "win": win, "pp": pp,
            "consts": consts, "rgw": rgw.reshape(128, 512), "hgn": hgn}


HALF = SEQ // 2
ARENA_COLS = 207 * 512


def wout_row_perm(g_list=(0, 1)):
    rows = []
    for g in g_list:
        for base, n in ((0, 2), (1536, 2), (512, 2), (1024, 2)):
            for u in range(n):
                r0 = base + (g * 2 + u) * 128
                rows.append(np.arange(r0, r0 + 128))
    return np.concatenate(rows)


def build_fused():
    nc = bass.Bass("TRN2", target_bir_lowering=False)
    din = lambda name, shape: nc.dram_tensor(name, shape, F32, kind="ExternalInput").ap()
    x_tok = din("x_tok", [HALF, D_MODEL])
    x_T = din("x_T", [D_MODEL, HALF])
    consts = din("consts", [128, NCONST])
    flags = din("flags", [128, 2])
    W = []
    for l in range(DEPTH):
        d = {}
        for i in range(2):
            d["wg%d" % i] = din("wg%d%d" % (l, i), [D_MODEL, D_FF])
            d["wu%d" % i] = din("wu%d%d" % (l, i), [D_MODEL, D_FF])
            d["wd%d" % i] = din("wd%d%d" % (l, i), [D_FF, D_MODEL])
        for i in range(3):
            d["lng%d" % i] = din("lng%d%d" % (l, i), [1, D_MODEL])
            d["lnb%d" % i] = din("lnb%d%d" % (l, i), [1, D_MODEL])
        d["wo"] = din("wo%d" % l, [D_MODEL, D_MODEL])
        d["win"] = din("win%d" % l, [D_MODEL, 2816])
        d["pp"] = din("pp%d" % l, [128, NPP])
        d["rgw"] = din("rgw%d" % l, [128, 512])
        d["hgn"] = din("hgn%d" % l, [1, 256])
        W.append(d)
    out = nc.dram_tensor("out", [HALF, D_MODEL], F32, kind="ExternalOutput").ap()
    scr = lambda name, shape, dt: nc.dram_tensor(name, shape, dt, kind="Internal").ap()
    R = [scr("xres%d" % i, [HALF, D_MODEL], F32) for i in range(2)]
    T = [[scr("xTb%d_%d" % (s, b), [D_MODEL, TB], BF16) for b in range(4)] for s in range(2)]
    XG = [scr("xg%d" % b, [2 * D_MODEL, TB], BF16) for b in range(4)]
    YS = [scr("ysrc%d" % g, [1024, GT], BF16) for g in range(8)]
    YG = [scr("yg%d" % g, [2048, GT], BF16) for g in range(8)]
    WSS = [[scr("ws%d_%d" % (k_, s_), [128, 8192], BF16) for s_ in range(34)] for k_ in range(3)]
    WSO = [scr("wso%d" % s_, [128, 8192], BF16) for s_ in range(4)]

    def precast_list(S, wg_, wu_, wd_, ws_):
        wg_v = wg_.rearrange("(kt p) c -> p kt c", p=128)
        wu_v = wu_.rearrange("(kt p) c -> p kt c", p=128)
        wd_v = wd_.rearrange("(kt p) c -> p kt c", p=128)
        lst = []
        for cp in range(22):
            for m, wv_ in enumerate((wg_v, wu_v)):
                lst.append(lambda cp=cp, m=m, wv_=wv_: S.dma("pool", lambda e: e.dma_start(
                    out=ws_[cp][:, m * 4096:(m + 1) * 4096].rearrange("p (kt c) -> p kt c", kt=16),
                    in_=wv_[:, :, cp * 256:(cp + 1) * 256]), writes=[("dram", "WSbg", id(ws_), cp, m)], slotq="poolbg"))
        for n in range(4):
            for ch, (k0, k1) in enumerate(((0, 16), (16, 32), (32, 44))):
                si = 22 + n * 3 + ch
                nk = k1 - k0
                lst.append(lambda si=si, nk=nk, k0=k0, k1=k1, n=n: S.dma("pool", lambda e: e.dma_start(
                    out=ws_[si][:, :nk * 512].rearrange("p (kt c) -> p kt c", kt=nk),
                    in_=wd_v[:, k0:k1, n * 512:(n + 1) * 512]), writes=[("dram", "WSbg", id(ws_), si)], slotq="poolbg"))
        return lst

    with ExitStack() as st:
        arena_t = st.enter_context(nc.sbuf_tensor("arena", [128, ARENA_COLS], BF16))
        sb = Arena(arena_t, ARENA_COLS)
        ps = [st.enter_context(nc.psum_tensor("ps%d" % i, [128, 512], F32)) for i in range(8)]
        S = Sched(nc)
        out_ops = []
        first = [True]

        def phase():
            if not first[0]:
                S.barrier()
            first[0] = False
            sb.reset()

        cur = None
        bg_store = {}
        for l in range(DEPTH):
            w = W[l]
            for i in range(2):
                phase()
                last = (l == DEPTH - 1 and i == 1)
                if cur is None:
                    xres_in, nxt = x_tok, 0
                    xT_v = x_T.rearrange("(kt p) t -> p kt t", p=128)
                    xT_src = lambda bt, xT_v=xT_v: ("pool", xT_v[:, :, bt * TB:(bt + 1) * TB], ("dram", "x_T"))
                else:
                    xres_in, nxt = R[cur], 1 - cur
                    xT_src = lambda bt, s=cur: ("sp", T[s][bt].rearrange("(kt p) t -> p kt t", p=128), ("dram", "T", s, bt))
                if last:
                    xo = lambda bt: None
                    out_res = out
                else:
                    out_res = R[nxt]
                    if i == 0:
                        xo = lambda bt, s=nxt: {"xTb": T[s][bt], "key": ("dram", "T", s, bt),
                                                "cc": (T[s][bt], XG[bt], ("dram", "XG", bt))}
                    else:
                        xo = lambda bt, s=nxt: {"xTb": T[s][bt], "key": ("dram", "T", s, bt), "cc": None}
                emit_ffn(S, sb, ps, {"ntok": HALF, "xres_in": xres_in, "xT_src": xT_src, "wg": w["wg%d" % i],
                                     "wu": w["wu%d" % i], "wd": w["wd%d" % i], "lng": w["lng%d" % (2 * i)],
                                     "lnb": w["lnb%d" % (2 * i)], "out_res": out_res, "xo": xo, "consts": consts,
                                     "out_ops": out_ops,
                                     "ws": WSS[1] if i == 1 else WSS[2 * (l % 2)],
                                     "ws_ready": (i == 1),
                                     "bg": bg_store.pop("ffn2", []) if i == 1 else []})
                cur = nxt
                if i == 1:
                    break
                phase()
                bg = precast_list(S, w["wg1"], w["wu1"], w["wd1"], WSS[1])
                bg_store["wout"] = []
                bg_store["ffn2"] = []
                wo_v_ = w["wo"].rearrange("(kt p) c -> p kt c", p=128)
                for n_ in range(4):
                    bg.append(lambda n_=n_, wo_v_=wo_v_: S.dma("pool", lambda e: e.dma_start(
                        out=WSO[n_].rearrange("p (kt c) -> p kt c", kt=16), in_=wo_v_[:, :, n_ * 512:(n_ + 1) * 512]),
                        writes=[("dram", "WSO", n_)], slotq="poolbg"))
                emit_mixer(S, sb, ps, {
                    "bg": bg, "bg_per_unit": (len(bg) + 63) // 64,
                    "seq": SEQ,
                    "xg": lambda gi: (XG[gi % 4].rearrange("(r kt p) t -> r p kt t", r=2, p=128)[gi // 4], ("dram", "XG", gi % 4)),
                    "win": w["win"], "pp": w["pp"], "consts": consts, "rgw": w["rgw"], "hgn": w["hgn"],
                    "ysrc": lambda gi: (YS[gi], ("dram", "YS", gi)),
                    "cc": lambda gi: (YS[gi], YG[gi], ("dram", "YG", gi)),
                    "out_ops": out_ops})
                phase()
                nxt = 1 - cur
                emit_wout(S, sb, ps, {"ntok": HALF, "xres_in": R[cur], "yg": lambda k: (YG[k], ("dram", "YG", k)),
                                      "wo": w["wo"], "lng": w["lng1"], "lnb": w["lnb1"], "out_res": R[nxt],
                                      "xo": lambda bt, s=nxt: {"xTb": T[s][bt], "key": ("dram", "T", s, bt), "cc": None},
                                      "consts": consts, "flags": flags, "out_ops": out_ops,
                                      "bg": bg_store.get("wout", []), "wso": WSO})
                cur = nxt
        S.barrier()
        S.wait_all("sp", out_ops)
        S.emit(st)
        print("fused instr counts", S.n_instr)
    return nc


_PROG = []


def kernel(x, ln_g, ln_b, ffn_w_gate, ffn_w_up, ffn_w_down, w_in, w_out, rg_conv_w, rg_conv_b, rg_w_a,
           rg_b_a, rg_w_x, rg_b_x, rg_lambda, cv_w, cv_b, cv_ln_g, cv_ln_b, hg_lower_bounds, hg_norm_g):
    f = lambda a: np.ascontiguousarray(np.asarray(a, dtype=np.float32))
    x = f(x)
    (ln_g, ln_b, ffn_w_gate, ffn_w_up, ffn_w_down, w_in, w_out, rg_conv_w, rg_conv_b, rg_w_a, rg_b_a, rg_w_x,
     rg_b_x, rg_lambda, cv_w, cv_b, cv_ln_g, cv_ln_b, hg_lower_bounds, hg_norm_g) = [f(a) for a in (
         ln_g, ln_b, ffn_w_gate, ffn_w_up, ffn_w_down, w_in, w_out, rg_conv_w, rg_conv_b, rg_w_a, rg_b_a, rg_w_x,
         rg_b_x, rg_lambda, cv_w, cv_b, cv_ln_g, cv_ln_b, hg_lower_bounds, hg_norm_g)]
    if not _PROG:
        _PROG.append(build_fused())
    nc = _PROG[0]
    consts = make_consts()
    perm = wout_row_perm()
    shared = {"consts": consts}
    for l in range(DEPTH):
        for i in range(2):
            shared["wg%d%d" % (l, i)] = ffn_w_gate[l, i]
            shared["wu%d%d" % (l, i)] = ffn_w_up[l, i]
            shared["wd%d%d" % (l, i)] = ffn_w_down[l, i]
        for i in range(3):
            shared["lng%d%d" % (l, i)] = ln_g[l, i].reshape(1, -1)
            shared["lnb%d%d" % (l, i)] = ln_b[l, i].reshape(1, -1)
        shared["wo%d" % l] = f(w_out[l][perm])
    in_maps = []
    for c in range(N_CORES):
        b, r = c // 2, c % 2
        xs_ = x[b, r * HALF:(r + 1) * HALF]
        m = dict(shared)
        m["x_tok"] = f(xs_)
        m["x_T"] = f(xs_.T)
        fl = np.zeros((128, 2), np.float32)
        fl[:, r] = 1.0
        m["flags"] = fl
        for l in range(DEPTH):
            mi = mixer_inputs(None, r, l, w_in, rg_conv_w, rg_conv_b, rg_w_a, rg_b_a, rg_w_x, rg_b_x, rg_lambda,
                              cv_w, cv_b, cv_ln_g, cv_ln_b, hg_lower_bounds, hg_norm_g, consts)
            m["win%d" % l] = mi["win"]
            m["pp%d" % l] = mi["pp"]
            m["rgw%d" % l] = mi["rgw"]
            m["hgn%d" % l] = mi["hgn"]
        in_maps.append(m)
    res = run_bass_kernel_spmd(nc, in_maps, core_ids=list(range(N_CORES)))
    out = np.empty((BATCH, SEQ, D_MODEL), np.float32)
    for c in range(N_CORES):
        out[c // 2, (c % 2) * HALF:(c % 2 + 1) * HALF] = res.results[c]["out"]
    return out
```
